# Optimizing a Trainium2 kernel written in Bass

```python
import jax, jax.numpy as jnp
from jax import lax
import numpy as np

D_MODEL = 1024
BATCH = 8
SEQ = 2048
DEPTH = 4

F32 = jnp.float32
CTX_LEN = 256
GRID_W = 64
N_EVEN = (DEPTH + 1) // 2
N_ODD = DEPTH // 2
RMS_EPS = 1e-6
NEG_BIG = -1e30
F_MIN = 1e-30

A_WIDTH = D_MODEL
A_HEADS = 8
A_BLOCK = A_WIDTH // A_HEADS
A_CONV = 4
A_C = 8.0
B_HEADS = 16
B_KV_HEADS = 4
B_HEAD_DIM = 64
B_GROUP = B_HEADS // B_KV_HEADS
B_WIDTH = B_HEADS * B_HEAD_DIM
B_KV_WIDTH = B_KV_HEADS * B_HEAD_DIM
WINDOW = 128
ROPE_BASE = 10000.0
ATTN_SCALE = B_HEAD_DIM ** -0.5
C_EXPAND = 128
C_HEADS = D_MODEL // C_EXPAND
C_FDIM = C_HEADS * C_EXPAND
C_VDIM = D_MODEL
C_HEAD_V = C_VDIM // C_HEADS
C_CHUNK = 64

EVEN_IN = 2 * A_WIDTH + 2 * B_WIDTH + 2 * B_KV_WIDTH
EVEN_MIX = A_WIDTH + B_WIDTH
ODD_IN = 3 * C_FDIM + 2 * C_VDIM

kernel_name = 'hybrid_rglru_swa_hgrn2_dit'


def rmsnorm(x, w):
    xf = x.astype(F32)
    y = xf * lax.rsqrt(jnp.mean(xf * xf, axis=-1, keepdims=True) + RMS_EPS)
    return (y * w.astype(F32)).astype(x.dtype)


def modulate(h, shift, scale):
    return h * (1 + scale) + shift


def axial_rope_tables(n_tokens):
    rows = n_tokens // GRID_W
    row = jnp.repeat(jnp.arange(rows), GRID_W).astype(F32)
    col = jnp.tile(jnp.arange(GRID_W), rows).astype(F32)
    axis_dim = B_HEAD_DIM // 2
    inv = ROPE_BASE ** (-jnp.arange(0, axis_dim, 2, dtype=F32) / axis_dim)
    ang = jnp.concatenate([row[:, None] * inv, col[:, None] * inv], axis=-1)
    return jnp.cos(ang), jnp.sin(ang)


def apply_axial_rope(t, cos, sin):
    c = cos[None, :, None, :].astype(t.dtype)
    s = sin[None, :, None, :].astype(t.dtype)
    t1, t2 = t[..., 0::2], t[..., 1::2]
    return jnp.stack([t1 * c - t2 * s, t1 * s + t2 * c], axis=-1).reshape(t.shape)


def centred_dwconv(x, w, b):
    L = x.shape[1]
    left = (A_CONV - 1) // 2
    xp = jnp.pad(x, ((0, 0), (left, A_CONV - 1 - left), (0, 0)))
    y = b
    for k in range(A_CONV):
        y = y + xp[:, k:k + L] * w[k]
    return y


def block_diag_linear(x, w, b):
    xb = x.reshape(x.shape[0], x.shape[1], A_HEADS, A_BLOCK)
    return jnp.einsum('blhi,hij->blhj', xb, w).reshape(x.shape) + b


def rglru_coeffs(u, wx, bx, wa, ba, lam):
    gate_x = jax.nn.sigmoid(block_diag_linear(u, wx, bx))
    gate_a = jax.nn.sigmoid(block_diag_linear(u, wa, ba))
    log_a = -A_C * gate_a * jax.nn.softplus(-lam.astype(F32))
    mult = jnp.sqrt(-jnp.expm1(2 * log_a))
    return jnp.exp(log_a), mult * gate_x * u


def linear_scan(a, b, h0, reverse):
    def combine(e1, e2):
        a1, b1 = e1
        a2, b2 = e2
        return a1 * a2, a2 * b1 + b2
    a_cum, b_cum = lax.associative_scan(combine, (a, b), reverse=reverse, axis=1)
    return b_cum + a_cum * h0[:, None, :]


def rglru_mixer(xa_ctx, xa_lat, conv_w, conv_b, wx, bx, wa, ba, lam, with_ctx):
    dt = xa_lat.dtype
    u_ctx = centred_dwconv(xa_ctx, conv_w, conv_b).astype(F32)
    u_lat = centred_dwconv(xa_lat, conv_w, conv_b).astype(F32)
    zeros = jnp.zeros((u_ctx.shape[0], A_WIDTH), F32)
    y_ctx, y_lat = [], []
    for d, rev in enumerate((False, True)):
        a_c, b_c = rglru_coeffs(u_ctx, wx[d], bx[d], wa[d], ba[d], lam[d])
        h_c = linear_scan(a_c, b_c, zeros, rev)
        h0 = h_c[:, 0] if rev else h_c[:, -1]
        a_l, b_l = rglru_coeffs(u_lat, wx[d], bx[d], wa[d], ba[d], lam[d])
        y_lat.append(linear_scan(a_l, b_l, h0, rev))
        if with_ctx:
            y_ctx.append(h_c)
    out_ctx = (y_ctx[0] + y_ctx[1]).astype(dt) if with_ctx else None
    return out_ctx, (y_lat[0] + y_lat[1]).astype(dt)


def latent_window_attention(q, k, v, k_ctx, v_ctx, sink):
    Bsz, S = q.shape[0], q.shape[1]
    nb = S // WINDOW
    qb = q.reshape(Bsz, nb, WINDOW, B_KV_HEADS, B_GROUP, B_HEAD_DIM)

    def band(t):
        tp = jnp.pad(t, ((0, 0), (WINDOW, WINDOW), (0, 0), (0, 0)))
        tp = tp.reshape(Bsz, nb + 2, WINDOW, B_KV_HEADS, B_HEAD_DIM)
        return jnp.concatenate([tp[:, :-2], tp[:, 1:-1], tp[:, 2:]], axis=2)

    kw, vw = band(k), band(v)
    s_win = jnp.einsum('bnqhgd,bnkhd->bnhgqk', qb, kw).astype(F32) * ATTN_SCALE
    blk = jnp.arange(nb)[:, None, None] * WINDOW
    q_pos = blk + jnp.arange(WINDOW)[None, :, None]
    k_pos = blk - WINDOW + jnp.arange(3 * WINDOW)[None, None, :]
    valid = (jnp.abs(q_pos - k_pos) <= WINDOW) & (k_pos >= 0) & (k_pos < S)
    s_win = jnp.where(valid[None, :, None, None], s_win, NEG_BIG)
    s_ctx = jnp.einsum('bnqhgd,bchd->bnhgqc', qb, k_ctx).astype(F32) * ATTN_SCALE
    snk = sink.astype(F32).reshape(1, 1, B_KV_HEADS, B_GROUP, 1, 1)
    m = jnp.maximum(jnp.maximum(s_win.max(-1, keepdims=True), s_ctx.max(-1, keepdims=True)), snk)
    p_win = jnp.exp(s_win - m)
    p_ctx = jnp.exp(s_ctx - m)
    denom = p_win.sum(-1, keepdims=True) + p_ctx.sum(-1, keepdims=True) + jnp.exp(snk - m)
    o = (jnp.einsum('bnhgqk,bnkhd->bnhgqd', p_win, vw.astype(F32))
         + jnp.einsum('bnhgqc,bchd->bnhgqd', p_ctx, v_ctx.astype(F32))) / denom
    return o.transpose(0, 1, 4, 2, 3, 5).reshape(Bsz, S, B_WIDTH).astype(q.dtype)


def context_attention(q, k, v, sink):
    Bsz, L = q.shape[0], q.shape[1]
    qc = q.reshape(Bsz, L, B_KV_HEADS, B_GROUP, B_HEAD_DIM)
    s = jnp.einsum('bqhgd,bkhd->bhgqk', qc, k).astype(F32) * ATTN_SCALE
    snk = jnp.broadcast_to(sink.astype(F32).reshape(1, B_KV_HEADS, B_GROUP, 1, 1), s.shape[:-1] + (1,))
    p = jax.nn.softmax(jnp.concatenate([s, snk], axis=-1), axis=-1)[..., :-1]
    o = jnp.einsum('bhgqk,bkhd->bqhgd', p, v.astype(F32))
    return o.reshape(Bsz, L, B_WIDTH).astype(q.dtype)


def even_mixer(u_ctx, u_lat, cos, sin, w_in, conv_w, conv_b, rg_wx, rg_bx, rg_wa, rg_ba, rg_lam,
               sink, w_out, with_ctx):
    n_state = A_WIDTH + 2 * B_KV_WIDTH
    p_lat = u_lat @ w_in
    p_ctx = u_ctx @ (w_in if with_ctx else w_in[:, :n_state])

    def kv(p):
        sh = p.shape[:2] + (B_KV_HEADS, B_HEAD_DIM)
        return (p[..., A_WIDTH:A_WIDTH + B_KV_WIDTH].reshape(sh),
                p[..., A_WIDTH + B_KV_WIDTH:n_state].reshape(sh))

    def gates_q(p):
        o = n_state
        ga = p[..., o:o + A_WIDTH]
        q = p[..., o + A_WIDTH:o + A_WIDTH + B_WIDTH].reshape(p.shape[:2] + (B_HEADS, B_HEAD_DIM))
        gb = p[..., o + A_WIDTH + B_WIDTH:]
        return ga, q, gb

    k_c, v_c = kv(p_ctx)
    k_l, v_l = kv(p_lat)
    ga_l, q_l, gb_l = gates_q(p_lat)
    q_l = apply_axial_rope(q_l, cos, sin)
    k_l = apply_axial_rope(k_l, cos, sin)
    ya_c, ya_l = rglru_mixer(p_ctx[..., :A_WIDTH], p_lat[..., :A_WIDTH], conv_w, conv_b,
                             rg_wx, rg_bx, rg_wa, rg_ba, rg_lam, with_ctx)
    yb_l = latent_window_attention(q_l, k_l, v_l, k_c, v_c, sink)
    out_l = jnp.concatenate([ya_l * jax.nn.silu(ga_l), yb_l * jax.nn.silu(gb_l)], axis=-1) @ w_out
    out_c = None
    if with_ctx:
        ga_c, q_c, gb_c = gates_q(p_ctx)
        yb_c = context_attention(q_c, k_c, v_c, sink)
        out_c = jnp.concatenate([ya_c * jax.nn.silu(ga_c), yb_c * jax.nn.silu(gb_c)], axis=-1) @ w_out
    return out_c, out_l


def gla_chunk_scan(q, k, g, v, s0):
    Bsz, L, H, _ = k.shape
    dv = v.shape[-1]
    n = L // C_CHUNK

    def chunks(t):
        return t.astype(F32).reshape(Bsz, n, C_CHUNK, H, t.shape[-1]).transpose(1, 0, 3, 2, 4)

    lower = jnp.tril(jnp.ones((C_CHUNK, C_CHUNK), dtype=bool))[:, :, None]

    def advance(S, kc, gc, vc):
        b = jnp.cumsum(gc, axis=2)
        b_last = b[:, :, -1:]
        S_new = (jnp.exp(b_last)[:, :, 0, :, None] * S
                 + jnp.einsum('bhsd,bhse->bhde', kc * jnp.exp(b_last - b), vc))
        return b, S_new

    if q is None:
        def step_state(S, inp):
            _, S_new = advance(S, *inp)
            return S_new, None
        s_fin, _ = lax.scan(step_state, s0, (chunks(k), chunks(g), chunks(v)))
        return None, s_fin

    def step(S, inp):
        qc, kc, gc, vc = inp
        b, S_new = advance(S, kc, gc, vc)
        decay = jnp.exp(jnp.where(lower, b[:, :, :, None] - b[:, :, None], NEG_BIG))
        attn = jnp.einsum('bhtd,bhsd,bhtsd->bhts', qc, kc, decay)
        o = (jnp.einsum('bhtd,bhde->bhte', qc * jnp.exp(b), S)
             + jnp.einsum('bhts,bhse->bhte', attn, vc))
        return S_new, o

    s_fin, o = lax.scan(step, s0, (chunks(q), chunks(k), chunks(g), chunks(v)))
    return o.transpose(1, 0, 3, 2, 4).reshape(Bsz, L, H, dv), s_fin


def maybe_flip(t, rev):
    return jnp.flip(t, axis=1) if rev else t


def odd_mixer(u_ctx, u_lat, w_in, lb, gnorm_w, w_out, with_ctx):
    dt = u_lat.dtype
    n_state = 2 * C_FDIM + C_VDIM
    p_lat = u_lat @ w_in
    p_ctx = u_ctx @ (w_in if with_ctx else w_in[:, :n_state])
    lb = lb.reshape(C_HEADS, C_EXPAND)

    def heads(t):
        return t.reshape(t.shape[0], t.shape[1], C_HEADS, -1)

    def forget(z):
        z = heads(z).astype(F32)
        f = lb + (1 - lb) * jax.nn.sigmoid(z)
        return (1 - lb) * jax.nn.sigmoid(-z), jnp.log(jnp.maximum(f, F_MIN))

    def state_parts(p):
        return p[..., :C_FDIM], p[..., C_FDIM:2 * C_FDIM], heads(p[..., 2 * C_FDIM:n_state])

    def out_parts(p):
        return jax.nn.silu(heads(p[..., n_state:n_state + C_FDIM])), p[..., n_state + C_FDIM:]

    ff_c, fb_c, v_c = state_parts(p_ctx)
    ff_l, fb_l, v_l = state_parts(p_lat)
    q_l, og_l = out_parts(p_lat)
    q_c, og_c = out_parts(p_ctx) if with_ctx else (None, None)
    zeros = jnp.zeros((u_lat.shape[0], C_HEADS, C_EXPAND, C_HEAD_V), F32)
    o_ctx, o_lat = [], []
    for f_c, f_l, rev in ((ff_c, ff_l, False), (fb_c, fb_l, True)):
        k_c, g_c = forget(f_c)
        k_l, g_l = forget(f_l)
        oc, s_c = gla_chunk_scan(maybe_flip(q_c, rev) if with_ctx else None, maybe_flip(k_c, rev),
                                 maybe_flip(g_c, rev), maybe_flip(v_c, rev), zeros)
        ol, _ = gla_chunk_scan(maybe_flip(q_l, rev), maybe_flip(k_l, rev), maybe_flip(g_l, rev),
                               maybe_flip(v_l, rev), s_c)
        o_lat.append(maybe_flip(ol, rev))
        if with_ctx:
            o_ctx.append(maybe_flip(oc, rev))

    def readout(o, og):
        y = rmsnorm(o, gnorm_w).reshape(o.shape[0], o.shape[1], C_VDIM) * jax.nn.silu(og.astype(F32))
        return y.astype(dt) @ w_out

    out_c = readout(o_ctx[0] + o_ctx[1], og_c) if with_ctx else None
    return out_c, readout(o_lat[0] + o_lat[1], og_l)


def setup_inputs(seed: int = 0) -> dict:
    key = jax.random.key(seed)
    ks = jax.random.split(key, 24)
    D = D_MODEL

    def nrm(k, shape, s):
        return jax.random.normal(k, shape, F32) * s

    u = jax.random.uniform(ks[14], (N_EVEN, 2, A_WIDTH), F32, 0.9, 0.999)
    sig = u ** (1.0 / A_C)
    return {
        'x': nrm(ks[0], (BATCH, SEQ, D), 1.0),
        'c': nrm(ks[1], (BATCH, D), 1.0),
        'ctx': nrm(ks[2], (BATCH, CTX_LEN, D), 1.0),
        'c_ctx': nrm(ks[3], (D,), 1.0),
        'ada_w': nrm(ks[4], (DEPTH, D, 3 * D), 0.5 * D ** -0.5),
        'ada_b': nrm(ks[5], (DEPTH, 3 * D), 0.02),
        'norm_w': 1.0 + nrm(ks[6], (DEPTH, D), 0.05),
        'ev_w_in': nrm(ks[7], (N_EVEN, D, EVEN_IN), D ** -0.5),
        'ev_conv_w': nrm(ks[8], (N_EVEN, A_CONV, A_WIDTH), A_CONV ** -0.5),
        'ev_conv_b': nrm(ks[9], (N_EVEN, A_WIDTH), 0.02),
        'ev_rg_wx': nrm(ks[10], (N_EVEN, 2, A_HEADS, A_BLOCK, A_BLOCK), A_BLOCK ** -0.5),
        'ev_rg_bx': nrm(ks[11], (N_EVEN, 2, A_WIDTH), 0.02),
        'ev_rg_wa': nrm(ks[12], (N_EVEN, 2, A_HEADS, A_BLOCK, A_BLOCK), A_BLOCK ** -0.5),
        'ev_rg_ba': nrm(ks[13], (N_EVEN, 2, A_WIDTH), 0.02),
        'ev_rg_lambda': jnp.log(sig) - jnp.log1p(-sig),
        'ev_sink': nrm(ks[15], (N_EVEN, B_HEADS), 0.5),
        'ev_w_out': nrm(ks[16], (N_EVEN, EVEN_MIX, D), EVEN_MIX ** -0.5),
        'od_w_in': nrm(ks[17], (N_ODD, D, ODD_IN), D ** -0.5),
        'od_lb_raw': nrm(ks[18], (N_ODD, C_FDIM), 1.0),
        'od_gnorm_w': 1.0 + nrm(ks[19], (N_ODD, C_HEAD_V), 0.05),
        'od_w_out': nrm(ks[20], (N_ODD, C_VDIM, D), C_VDIM ** -0.5),
        'final_norm_w': 1.0 + nrm(ks[21], (D,), 0.05),
    }


def reference(x, c, ctx, c_ctx, ada_w, ada_b, norm_w, ev_w_in, ev_conv_w, ev_conv_b, ev_rg_wx,
              ev_rg_bx, ev_rg_wa, ev_rg_ba, ev_rg_lambda, ev_sink, ev_w_out, od_w_in, od_lb_raw,
              od_gnorm_w, od_w_out, final_norm_w):
    D = D_MODEL
    cos, sin = axial_rope_tables(x.shape[1])
    silu_c = jax.nn.silu(c)
    silu_cc = jax.nn.silu(c_ctx)
    lb_p = jax.nn.softmax(od_lb_raw.astype(F32), axis=0)
    lower_bounds = jnp.cumsum(lb_p, axis=0) - lb_p[0]
    h_lat, h_ctx = x, ctx
    for layer in range(DEPTH):
        last = layer == DEPTH - 1
        j = layer // 2
        mod_l = silu_c @ ada_w[layer] + ada_b[layer]
        sh_l, sc_l, gt_l = jnp.split(mod_l[:, None, :], 3, axis=-1)
        n_mod = 2 * D if last else 3 * D
        mod_c = silu_cc @ ada_w[layer][:, :n_mod] + ada_b[layer][:n_mod]
        u_lat = modulate(rmsnorm(h_lat, norm_w[layer]), sh_l, sc_l)
        u_ctx = modulate(rmsnorm(h_ctx, norm_w[layer]), mod_c[:D], mod_c[D:2 * D])
        if layer % 2 == 0:
            m_ctx, m_lat = even_mixer(u_ctx, u_lat, cos, sin, ev_w_in[j], ev_conv_w[j], ev_conv_b[j],
                                      ev_rg_wx[j], ev_rg_bx[j], ev_rg_wa[j], ev_rg_ba[j],
                                      ev_rg_lambda[j], ev_sink[j], ev_w_out[j], not last)
        else:
            m_ctx, m_lat = odd_mixer(u_ctx, u_lat, od_w_in[j], lower_bounds[j], od_gnorm_w[j],
                                     od_w_out[j], not last)
        h_lat = h_lat + gt_l * m_lat
        if not last:
            h_ctx = h_ctx + mod_c[2 * D:] * m_ctx
    return rmsnorm(h_lat, final_norm_w)
```

```python
import os
import numpy as np
import concourse.bass as bass
import concourse.mybir as mybir
from concourse.bass_utils import run_bass_kernel_spmd

F32 = mybir.dt.float32
BF16 = mybir.dt.bfloat16
AF = mybir.ActivationFunctionType
ALU = mybir.AluOpType

ENGS = ("tensor", "vector", "scalar", "gpsimd", "sync")
NLAYERS = 4
KSTOP = os.environ.get('KSTOP', '')
T = 2304
TT = [(0, 256), (256, 768), (768, 1280), (1280, 1792), (1792, 2304)]
SLOTW = 2320


class Op:
    __slots__ = ("eng", "fn", "deps", "needed", "lane", "val", "pos")

    def __init__(self, eng, fn, lane):
        self.pos = -1
        self.eng = eng
        self.fn = fn
        self.deps = []
        self.needed = False
        self.lane = lane
        self.val = None


class Sched:
    def __init__(self, nc):
        self.nc = nc
        self.q = {e: [] for e in ENGS}
        self.last_w = {}
        self.readers = {}
        self.lane_cnt = {}
        self.epoch = None

    @staticmethod
    def _is_slot(k):
        return len(k) > 1 and k[0] == "s" and k[1].isdigit()

    def fence(self, eng="sync"):
        op = Op(eng, lambda e: e.nop(), None)
        deps = {}
        for k in list(self.last_w.keys()):
            if self._is_slot(k):
                w = self.last_w.pop(k)
                deps[id(w)] = w
                for r in self.readers.pop(k, ()):
                    deps[id(r)] = r
        if self.epoch is not None:
            deps[id(self.epoch)] = self.epoch
        for d in self._prune(deps.values()):
            op.deps.append(d)
            d.needed = True
        op.pos = len(self.q[eng])
        self.q[eng].append(op)
        self.epoch = op
        return op

    @staticmethod
    def _prune(deps):
        best = {}
        out = []
        for d in deps:
            if d.lane is not None:
                out.append(d)
            elif d.eng not in best or d.pos > best[d.eng].pos:
                best[d.eng] = d
        return out + list(best.values())

    def add(self, eng, fn, reads=(), writes=(), lane=None):
        op = Op(eng, fn, lane)
        deps = {}
        if self.epoch is not None:
            for k in list(reads) + list(writes):
                if self._is_slot(k) and k not in self.last_w:
                    self.last_w[k] = self.epoch
        for k in reads:
            w = self.last_w.get(k)
            if w is not None:
                deps[id(w)] = w
            if k.startswith("ps"):
                for r in self.readers.get(k, ()):
                    if r.eng != eng:
                        deps[id(r)] = r
        for k in writes:
            w = self.last_w.get(k)
            if w is not None:
                deps[id(w)] = w
            for r in self.readers.get(k, ()):
                deps[id(r)] = r
        cand = [d for d in deps.values()
                if d is not op and not (d.eng == "tensor" and eng == "tensor" and d.lane is None)]
        for d in self._prune(cand):
            op.deps.append(d)
            d.needed = True
        for k in reads:
            self.readers.setdefault(k, []).append(op)
        for k in writes:
            self.last_w[k] = op
            self.readers[k] = []
        op.pos = len(self.q[eng])
        self.q[eng].append(op)
        if lane is not None:
            c = self.lane_cnt.get(lane, 0) + 1
            self.lane_cnt[lane] = c
            op.val = 16 * c
        return op

    def emit(self):
        nc = self.nc
        for e in ENGS:
            c = 0
            for op in self.q[e]:
                if op.lane is None and op.needed:
                    c += 1
                    op.val = c
        lanes = list(self.lane_cnt.keys())
        from contextlib import ExitStack
        with ExitStack() as st:
            esem = {e: st.enter_context(nc.semaphore("s_" + e)) for e in ENGS}
            lsem = {l: st.enter_context(nc.semaphore("l_%d" % i)) for i, l in enumerate(lanes)}
            block = st.enter_context(nc.Block())

            def semof(op):
                return lsem[op.lane] if op.lane is not None else esem[op.eng]

            def run(engname):
                def body(e):
                    known = {}
                    for op in self.q[engname]:
                        need = {}
                        for d in op.deps:
                            s = semof(d)
                            if d.val > need.get(s.num, (None, 0))[1]:
                                need[s.num] = (s, d.val)
                        for key, (s, v) in need.items():
                            if known.get(key, 0) >= v:
                                continue
                            e.wait_ge(s, v)
                            known[key] = v
                        ins = op.fn(e)
                        if op.lane is not None:
                            ins.then_inc(lsem[op.lane], 16)
                        elif op.needed:
                            ins.then_inc(esem[engname], 1)
                    if engname == "sync":
                        for l in lanes:
                            e.wait_ge(lsem[l], 16 * self.lane_cnt[l])
                        for en in ENGS:
                            tot = sum(1 for o in self.q[en] if o.lane is None and o.needed)
                            if tot:
                                e.wait_ge(esem[en], tot)
                return body

            block.tensor(run("tensor"))
            block.vector(run("vector"))
            block.scalar(run("scalar"))
            block.gpsimd(run("gpsimd"))
            block.sync(run("sync"))


def fm(v):
    v = np.asarray(v, np.float32)
    return np.ascontiguousarray(v.reshape(-1, 128).T)


def wtile(W):
    W = np.asarray(W, np.float32)
    w = W.shape[1]
    return W.reshape(8, 128, w).transpose(1, 0, 2).reshape(128, 8 * w)


PERM = np.concatenate([np.arange(0, 64, 2), np.arange(1, 64, 2)])


class Layout:
    def __init__(self):
        self.off = {}
        self.parts = []
        self.n = 0

    def put(self, name, arr):
        arr = np.ascontiguousarray(arr, np.float32)
        assert arr.shape[0] == 128
        self.off[name] = (self.n, arr.shape[1])
        self.parts.append(arr)
        self.n += arr.shape[1]

    def build(self):
        return np.ascontiguousarray(np.concatenate(self.parts, axis=1))


def build_weights(inp):
    L = Layout()
    for l in range(4):
        for t in range(6):
            L.put("ada%d_%d" % (l, t), wtile(inp["ada_w"][l][:, t * 512:(t + 1) * 512]))
    NS = 1536
    for j in range(2):
        W = inp["ev_w_in"][j]
        for i in range(4):
            cols = []
            for h in (2 * i, 2 * i + 1):
                cols += list(range(h * 128, h * 128 + 128))
                cols += list(range(NS + h * 128, NS + h * 128 + 128))
            L.put("evA%d_%d" % (j, i), wtile(W[:, cols]))
        for g in range(4):
            kc = list(1024 + g * 64 + PERM)
            qc = []
            for hh in range(4 * g, 4 * g + 4):
                qc += list(NS + 1024 + hh * 64 + PERM)
            vc = list(range(1024 + 256 + g * 64, 1024 + 256 + g * 64 + 64))
            L.put("evB1_%d_%d" % (j, g), wtile(W[:, kc + kc + qc + vc]))
            gc = list(range(NS + 2048 + g * 256, NS + 2048 + g * 256 + 256))
            L.put("evB2_%d_%d" % (j, g), wtile(W[:, gc]))
        Wo = inp["ev_w_out"][j]
        for half in range(2):
            for t in range(2):
                L.put("evO%d_%d_%d" % (j, half, t),
                      wtile(Wo[half * 1024:(half + 1) * 1024, t * 512:(t + 1) * 512]))
        for h in range(8):
            m = np.stack([inp["ev_rg_wx"][j][0][h], inp["ev_rg_wa"][j][0][h],
                          inp["ev_rg_wx"][j][1][h], inp["ev_rg_wa"][j][1][h]], axis=1)
            L.put("rg%d_%d" % (j, h), m.reshape(128, 512))
    for j in range(2):
        W = inp["od_w_in"][j]
        for hd in range(8):
            cols = []
            for base in (0, 1024, 2048, 3072):
                cols += list(range(base + hd * 128, base + hd * 128 + 128))
            L.put("od1_%d_%d" % (j, hd), wtile(W[:, cols]))
            L.put("od2_%d_%d" % (j, hd), wtile(W[:, 4096 + hd * 128:4096 + hd * 128 + 128]))
        Wo = inp["od_w_out"][j]
        for t in range(2):
            L.put("odO%d_%d" % (j, t), wtile(Wo[:, t * 512:(t + 1) * 512]))
    return L


def build_params(inp, b, L=None):
    L = Layout()
    L.put("c", fm(inp["c"][b]))
    L.put("cc", fm(inp["c_ctx"]))
    for l in range(4):
        L.put("nw%d" % l, fm(inp["norm_w"][l]))
        L.put("ab%d" % l, fm(inp["ada_b"][l]))
    L.put("fnw", fm(inp["final_norm_w"]))
    for j in range(2):
        cw = np.concatenate([fm(inp["ev_conv_w"][j][k]) for k in range(4)], axis=1)
        L.put("cw%d" % j, cw)
        L.put("cb%d" % j, fm(inp["ev_conv_b"][j]))
        for d in range(2):
            L.put("bx%d%d" % (j, d), fm(inp["ev_rg_bx"][j][d]))
            L.put("ba%d%d" % (j, d), fm(inp["ev_rg_ba"][j][d]))
            L.put("lam%d%d" % (j, d), fm(inp["ev_rg_lambda"][j][d]))
        sk = np.zeros((128, 8), np.float32)
        for g in range(4):
            for c2 in range(2):
                sk[64:, g * 2 + c2] = inp["ev_sink"][j][4 * g + 2 * c2]
                sk[:64, g * 2 + c2] = inp["ev_sink"][j][4 * g + 2 * c2 + 1]
        L.put("snk%d" % j, sk)
    L.put("lbr0", fm(inp["od_lb_raw"][0]))
    L.put("lbr1", fm(inp["od_lb_raw"][1]))
    for j in range(2):
        L.put("gn%d" % j, np.asarray(inp["od_gnorm_w"][j], np.float32).reshape(128, 1))
    L.put("one", np.ones((128, 1), np.float32))
    L.put("zero", np.zeros((128, 1), np.float32))
    return L


def build_consts():
    L = Layout()
    L.put("ident", np.eye(128, dtype=np.float32))
    P = np.zeros((128, 128), np.float32)
    for p in range(128):
        q = (p // 64) * 64 + ((p % 64) + 32) % 64
        P[p, q] = 1.0
    L.put("pswap", P)
    L.put("ones", np.ones((128, 128), np.float32))
    jj = np.arange(128)[:, None]
    ii = np.arange(128)[None, :]
    mprev = np.where(jj >= ii, 0.0, -30000.0).astype(np.float32)
    mnext = np.where(jj <= ii, 0.0, -30000.0).astype(np.float32)
    L.put("mprev", np.tile(mprev, (1, 4)))
    L.put("mnext", np.tile(mnext, (1, 4)))
    L.put("maskF", (jj <= ii).astype(np.float32))
    L.put("maskB", (jj >= ii).astype(np.float32))
    n = np.arange(2048)
    row = (n // 64).astype(np.float32)
    col = (n % 64).astype(np.float32)
    inv = (10000.0 ** (-np.arange(0, 32, 2, dtype=np.float32) / 32)).astype(np.float32)
    ang = np.concatenate([row[:, None] * inv, col[:, None] * inv], axis=-1)
    cos = np.cos(ang).astype(np.float32).T
    sin = np.sin(ang).astype(np.float32).T
    C = np.concatenate([cos, cos, cos, cos], axis=0)
    Sg = np.concatenate([-sin, sin, -sin, sin], axis=0)
    L.put("ropeC", C)
    L.put("ropeS", Sg)
    return L


def build_program(WL, PL, CL, nlayers):
    nc = bass.Bass("TRN2", target_bir_lowering=False)
    S = Sched(nc)
    h0_d = nc.dram_tensor("h0", [128, 8 * T], F32, kind="ExternalInput").ap()
    par_d = nc.dram_tensor("par", [128, PL.n], F32, kind="ExternalInput").ap()
    wall_d = nc.dram_tensor("wall", [128, WL.n], F32, kind="ExternalInput").ap()
    cst_d = nc.dram_tensor("cst", [128, CL.n], F32, kind="ExternalInput").ap()
    out_d = nc.dram_tensor("outT", [128, 8 * 2048], F32, kind="ExternalOutput").ap()
    mixD = nc.dram_tensor("mixD", [16, 128, T], BF16).ap()

    hT = nc.alloc_sbuf_tensor("hT", [128, 8 * T], F32)
    uT = nc.alloc_sbuf_tensor("uT", [128, 8 * T], BF16)
    slots = [nc.alloc_sbuf_tensor("slot%d" % i, [128, SLOTW], F32) for i in range(9)]
    par = nc.alloc_sbuf_tensor("par_sb", [128, PL.n], F32)
    NCB = CL.off["ropeC"][0]
    cstb = nc.alloc_sbuf_tensor("cst_sb", [128, NCB], BF16)
    modT = [nc.alloc_sbuf_tensor("modT%d" % i, [128, 48], F32) for i in range(4)]
    w1vs = [nc.alloc_sbuf_tensor("w1v%d" % i, [128, 16], F32) for i in range(4)]
    scT = nc.alloc_sbuf_tensor("scT", [128, 16], BF16)
    sm = nc.alloc_sbuf_tensor("small", [128, 768], F32)
    vtile_t = nc.alloc_sbuf_tensor("vtile", [128, 512], BF16)
    ktok_t = [nc.alloc_sbuf_tensor("ktok%d" % i, [64, 128], BF16) for i in range(6)]
    atm_t = [nc.alloc_sbuf_tensor("atm%d" % i, [64, 64], BF16) for i in range(6)]
    cstu = cstb[:, :].bitcast(mybir.dt.uint16)
    zb = nc.alloc_sbuf_tensor("zb", [128, 32], BF16)
    dummy = nc.alloc_sbuf_tensor("mkdummy", [128, 2], F32)
    rgw = [nc.alloc_sbuf_tensor("rgw%d" % i, [128, 512], BF16) for i in range(2)]
    Sst = [nc.alloc_sbuf_tensor("Sst%d" % i, [128, 128], F32) for i in range(2)]
    Stm = [nc.alloc_sbuf_tensor("Stm%d" % i, [128, 128], F32) for i in range(2)]
    Sbf = [nc.alloc_sbuf_tensor("Sbf%d" % i, [128, 128], BF16) for i in range(2)]
    ps = [nc.alloc_psum_tensor("ps%d" % i, [128, 512], F32) for i in range(8)]

    hT3 = hT[:, :].rearrange("p (k n) -> p k n", k=8)
    uT3 = uT[:, :].rearrange("p (k n) -> p k n", k=8)

    def P_(name, a=0, b=None):
        o, w = PL.off[name]
        b = w if b is None else b
        return par[:, o + a:o + b]

    def C_(name, a=0, b=None):
        o, w = CL.off[name]
        b = w if b is None else b
        return cstb[:, o + a:o + b]

    def sbf(i, a=0, b=2 * SLOTW):
        return slots[i][:, :].bitcast(BF16)[:, a:b]

    def sf(i, a=0, b=SLOTW):
        return slots[i][:, a:b]

    bank_ctr = [0]

    def bank():
        i = bank_ctr[0] % 6
        bank_ctr[0] += 1
        return "ps%d" % i, ps[i]

    def PE(fn, r, w):
        return S.add("tensor", fn, r, w)

    def V(fn, r, w):
        return S.add("vector", fn, r, w)

    def A(fn, r, w):
        return S.add("scalar", fn, r, w)

    def G(fn, r, w):
        return S.add("gpsimd", fn, r, w)

    def DMA(eng, out, in_, r, w, lane):
        return S.add(eng, lambda e: e.dma_start(out=out, in_=in_), r, w, lane=lane)

    def mm(out, lhsT, rhs, start, stop, r, w):
        return PE(lambda e: e.matmul(out, lhsT=lhsT, rhs=rhs, start=start, stop=stop), r, w)

    def act(out, in_, func, r, w, bias=None, scale=None):
        kw = {}
        if bias is not None:
            kw["bias"] = bias
        if scale is not None:
            kw["scale"] = scale
        return A(lambda e: e.activation(out=out, in_=in_, func=func, **kw), r, w)

    def tt(eng, out, in0, in1, op, r, w):
        return S.add(eng, lambda e: e.tensor_tensor(out=out, in0=in0, in1=in1, op=op), r, w)

    def ts(eng, out, in0, s1, s2, op0, op1, r, w):
        if op1 is None:
            return S.add(eng, lambda e: e.tensor_scalar(out=out, in0=in0, scalar1=s1, scalar2=None, op0=op0), r, w)
        return S.add(eng, lambda e: e.tensor_scalar(out=out, in0=in0, scalar1=s1, scalar2=s2, op0=op0, op1=op1), r, w)

    def stt(out, in0, scalar, in1, op0, op1, r, w):
        return V(lambda e: e.scalar_tensor_tensor(out=out, in0=in0, scalar=scalar, in1=in1, op0=op0, op1=op1), r, w)

    wctr = [0]

    def wload(name):
        o, n = WL.off[name]
        i = wctr[0] % 2
        wctr[0] += 1
        key = "w%d" % i
        dst = sbf(7 + i, 0, n)
        DMA("gpsimd", dst, wall_d[:, o:o + n], [], [key], key)
        return key, dst.rearrange("p (k c) -> p k c", k=8)

    DMA("sync", par[:, :], par_d, [], ["par"], "par")
    DMA("gpsimd", cstb[:, :], cst_d[:, 0:NCB], [], ["cst"], "cst")
    for k in range(8):
        DMA("sync", hT[:, k * T:(k + 1) * T], h0_d[:, k * T:(k + 1) * T], [],
            ["h%d" % i for i in range(5)], "h0_%d" % k)
    V(lambda e: e.memset(zb[:, :], 0.0), [], ["zb"])
    scT3 = scT[:, :].rearrange("p (k c) -> p k c", c=2)
    act(scT3[:, :, 0], P_("c"), AF.Silu, ["par"], ["scT"])
    act(scT3[:, :, 1], P_("cc"), AF.Silu, ["par"], ["scT"])
    ident = C_("ident")
    ones_bf = C_("ones")

    def ada_steps(l):
        mk = "mod%d" % l
        mT = modT[l]
        m3 = mT[:, :].rearrange("p (j c) -> p j c", c=2)
        ab = P_("ab%d" % l)
        w1v = w1vs[l]

        def step(t):
            pk, pb = bank()
            wk, wb = wload("ada%d_%d" % (l, t))
            for jj in range(4):
                for k in range(8):
                    mm(pb[:, 2 * jj:2 * jj + 2], wb[:, k, jj * 128:(jj + 1) * 128], scT3[:, k, :],
                       k == 0, k == 7, [wk, "scT"], [pk])
            tt("vector", m3[:, 4 * t:4 * t + 4, :], pb[:, 0:8].rearrange("p (j c) -> p j c", c=2),
               ab[:, 4 * t:4 * t + 4].unsqueeze(2).to_broadcast([128, 4, 2]), ALU.add, [pk, "par"], [mk])

        def fin():
            for c in range(2):
                ts("vector", w1v[:, c * 8:(c + 1) * 8], m3[:, 8:16, c], 1.0, None, ALU.add, None, [mk], ["w1v%d" % l])
                tt("vector", w1v[:, c * 8:(c + 1) * 8], w1v[:, c * 8:(c + 1) * 8], P_("nw%d" % l), ALU.mult,
                   ["w1v%d" % l, "par"], ["w1v%d" % l])
        return [lambda t=t: step(t) for t in range(6)] + [fin]

    ada_info = {}

    def ada_get(l):
        mT = modT[l]
        return mT[:, :].rearrange("p (j c) -> p j c", c=2), "mod%d" % l

    pending_ada = []

    def ada_pump(n=1):
        for _ in range(n):
            if pending_ada:
                pending_ada.pop(0)()

    def norm(l, m3, mk):
        for ti, (a, b) in enumerate(TT):
            w = b - a
            c = 1 if ti == 0 else 0
            sq = sbf(ti % 2, 0, 8 * 512).rearrange("p (k n) -> p k n", k=8)
            sqk = "s%d" % (ti % 2)
            act(sq[:, :, :w], hT3[:, :, a:b], AF.Square, ["h%d" % ti], [sqk])
            pk, pb = bank()
            for k in range(8):
                mm(pb[:, :w], ones_bf, sq[:, k, :w], k == 0, k == 7, [sqk, "cst"], [pk])
            rs = sf(2, (ti % 2) * 512, (ti % 2) * 512 + w)
            rk = "s2r%d" % (ti % 2)
            ts("vector", rs, pb[:, :w], 1.0 / 1024, 1e-6, ALU.mult, ALU.add, [pk], [rk])
            act(rs, rs, AF.Sqrt, [rk], [rk])
            V(lambda e, rs=rs: e.reciprocal(out=rs, in_=rs), [rk], [rk])
            for k in range(8):
                tmp = sf(3, (k % 2) * 512, (k % 2) * 512 + w)
                tk = "s3t%d" % (k % 2)
                stt(tmp, hT3[:, k, a:b], w1vs[l][:, c * 8 + k:c * 8 + k + 1], rs, ALU.mult, ALU.mult,
                    ["h%d" % ti, "w1v%d" % l, rk], [tk])
                act(uT3[:, k, a:b], tmp, AF.Identity, [tk, mk], ["u%d" % ti], bias=m3[:, k, c:c + 1])

    def proj(wb, wk, col0, M, ti, pb, pk, width=None, a0=None):
        a, b = TT[ti]
        if a0 is not None:
            a, b = a0, a0 + width
        for k in range(8):
            mm(pb[0:M, :b - a], wb[:, k, col0:col0 + M], uT3[:, k, a:b], k == 0, k == 7,
               [wk, "u%d" % ti], [pk])

    def outproj(names, k0, m3, mk):
        wts = [wload(n) for n in names]
        for ti, (a, b) in enumerate(TT):
            w = b - a
            c = 1 if ti == 0 else 0
            mr = sbf(ti % 2, 0, 8 * 512).rearrange("p (k n) -> p k n", k=8)
            mrk = "s%d" % (ti % 2)
            DMA("sync", mr[:, :, :w], mixD[k0:k0 + 8].rearrange("k p n -> p k n")[:, :, a:b],
                ["mixD"], [mrk], mrk)
            for j in range(8):
                wk, wb = wts[j // 4]
                pk, pb = bank()
                for k in range(8):
                    mm(pb[:, :w], wb[:, k, (j % 4) * 128:(j % 4 + 1) * 128], mr[:, k, :w], k == 0, k == 7,
                       [wk, mrk], [pk])
                stt(hT3[:, j, a:b], pb[:, :w], m3[:, 16 + j, c:c + 1], hT3[:, j, a:b], ALU.mult, ALU.add,
                    [pk, mk, "h%d" % ti], ["h%d" % ti])

    SEG = [(1, 257)] + [(260 + 512 * i, 260 + 512 * (i + 1)) for i in range(4)]

    def rg_setup(j):
        for d in range(2):
            t0 = sm[:, 32 + d * 8:40 + d * 8]
            act(t0, P_("lam%d%d" % (j, d)), AF.Exp, ["par"], ["smt"], scale=-1.0)
            ts("vector", t0, t0, 1.0, None, ALU.add, None, ["smt"], ["smt"])
            act(t0, t0, AF.Ln, ["smt"], ["smt"])
            ts("vector", sm[:, d * 16:d * 16 + 8], t0, -8.0, None, ALU.mult, None, ["smt"], ["rgc"])
            ts("vector", sm[:, 640 + d * 8:648 + d * 8], t0, -4.0, None, ALU.mult, None, ["smt"], ["rgc"])
            ts("vector", sm[:, 656 + d * 8:664 + d * 8], P_("bx%d%d" % (j, d)), 0.5, None, ALU.mult, None, ["par"], ["rgc"])
            ts("vector", sm[:, 672 + d * 8:680 + d * 8], P_("ba%d%d" % (j, d)), 0.5, None, ALU.mult, None, ["par"], ["rgc"])
        V(lambda e: e.memset(sm[:, 690:691], 0.25), [], ["rgc"])

    def rglru(j, h, wk, wb, colx, colg):
        xa, uc, hf, hb = sf(0), sf(1), sf(3), sf(4)
        ucb = sbf(2, 0, SLOTW)
        mst = sbf(2, SLOTW, SLOTW + T)
        rk, rw = "rgw%d" % (h % 2), rgw[h % 2]
        o, n = WL.off["rg%d_%d" % (j, h)]
        DMA("gpsimd", rw[:, :], wall_d[:, o:o + n], [], [rk], rk)
        rw3 = rw[:, :].rearrange("p (m c) -> p m c", m=4)
        for (a, b) in ((0, 1), (257, 260), (2308, 2312)):
            V(lambda e, a=a, b=b: e.memset(xa[:, a:b], 0.0), [], ["s0"])
        for ti in range(5):
            pk, pb = bank()
            proj(wb, wk, colx, 128, ti, pb, pk)
            a, b = SEG[ti]
            V(lambda e, a=a, b=b, pb=pb: e.tensor_copy(out=xa[:, a:b], in_=pb[:, :b - a]), [pk], ["s0"])
        cw = P_("cw%d" % j)
        ts("vector", uc[:, 1:2308], xa[:, 0:2307], cw[:, h:h + 1], P_("cb%d" % j, h, h + 1), ALU.mult, ALU.add,
           ["s0", "par"], ["s1"])
        for tap in range(1, 4):
            stt(uc[:, 1:2308], xa[:, tap:tap + 2307], cw[:, tap * 8 + h:tap * 8 + h + 1], uc[:, 1:2308],
                ALU.mult, ALU.add, ["s0", "par", "s1"], ["s1"])
        act(ucb[:, 1:2308], uc[:, 1:2308], AF.Copy, ["s1"], ["s2a"])
        RNG = ((1, 257), (260, 2308))
        thx, aa, mt = sf(0), sf(5), sf(6)
        for d in range(2):
            hd, hk = (hf, "s3") if d == 0 else (hb, "s4")
            c1 = sm[:, d * 16 + h:d * 16 + h + 1]
            hc1 = sm[:, 640 + d * 8 + h:641 + d * 8 + h]
            hbx = sm[:, 656 + d * 8 + h:657 + d * 8 + h]
            hba = sm[:, 672 + d * 8 + h:673 + d * 8 + h]
            for (a, b) in SEG:
                w = b - a
                pkx, pbx = bank()
                mm(pbx[:, :w], rw3[:, 2 * d, :], ucb[:, a:b], True, True, [rk, "s2a"], [pkx])
                pka, pba = bank()
                mm(pba[:, :w], rw3[:, 2 * d + 1, :], ucb[:, a:b], True, True, [rk, "s2a"], [pka])
                act(thx[:, a:b], pbx[:, :w], AF.Tanh, [pkx, "rgc"], ["s0"], scale=0.5, bias=hbx)
                act(aa[:, a:b], pba[:, :w], AF.Tanh, [pka, "rgc"], ["s5"], scale=0.5, bias=hba)
                act(mt[:, a:b], aa[:, a:b], AF.Exp, ["s5", "rgc"], ["s6"], scale=c1, bias=c1)
                act(aa[:, a:b], aa[:, a:b], AF.Exp, ["s5", "rgc"], ["s5"], scale=hc1, bias=hc1)
            for (a, b) in RNG:
                act(mt[:, a:b], mt[:, a:b], AF.Sqrt, ["s6", "rgc"], ["s6"], scale=-0.25, bias=sm[:, 690:691])
                stt(thx[:, a:b], thx[:, a:b], 1.0, mt[:, a:b], ALU.add, ALU.mult, ["s0", "s6"], ["s0"])
                tt("vector", thx[:, a:b], thx[:, a:b], uc[:, a:b], ALU.mult, ["s0", "s1"], ["s0"])
            if d == 0:
                V(lambda e: e.tensor_tensor_scan(out=hf[:, 1:257], data0=aa[:, 1:257], data1=thx[:, 1:257],
                                                 initial=0.0, op0=ALU.mult, op1=ALU.add), ["s5", "s0"], ["s3"])
                V(lambda e: e.tensor_tensor_scan(out=hf[:, 260:2308], data0=aa[:, 260:2308], data1=thx[:, 260:2308],
                                                 initial=hf[:, 256:257], op0=ALU.mult, op1=ALU.add),
                  ["s5", "s0", "s3"], ["s3"])
            else:
                V(lambda e: e.tensor_tensor_scan(out=hb[:, 256:0:-1], data0=aa[:, 256:0:-1], data1=thx[:, 256:0:-1],
                                                 initial=0.0, op0=ALU.mult, op1=ALU.add), ["s5", "s0"], ["s4"])
                V(lambda e: e.tensor_tensor_scan(out=hb[:, 2307:259:-1], data0=aa[:, 2307:259:-1],
                                                 data1=thx[:, 2307:259:-1], initial=hb[:, 1:2],
                                                 op0=ALU.mult, op1=ALU.add), ["s5", "s0", "s4"], ["s4"])
        tt("vector", hb[:, 1:257], hb[:, 1:257], hf[:, 1:257], ALU.add, ["s3", "s4"], ["s4"])
        tt("vector", hb[:, 260:2308], hb[:, 260:2308], hf[:, 260:2308], ALU.add, ["s3", "s4"], ["s4"])
        for ti in range(5):
            pk, pb = bank()
            proj(wb, wk, colg, 128, ti, pb, pk)
            a, b = SEG[ti]
            ta, tb = TT[ti]
            w = b - a
            th = sf(5, (ti % 2) * 512, (ti % 2) * 512 + w)
            thk = "s5"
            act(th, pb[:, :w], AF.Tanh, [pk], [thk], scale=0.5)
            stt(th, th, 1.0, pb[:, :w], ALU.add, ALU.mult, [pk, thk], [thk])
            stt(mst[:, ta:tb], th, 0.5, hb[:, a:b], ALU.mult, ALU.mult, [thk, "s4"], ["s2b"])
        DMA("sync", mixD[h], mst, ["s2b"], ["mixD"], "mixst")

    def attn_setup(j):
        oC, nC = CL.off["ropeC"]
        oS, nS = CL.off["ropeS"]
        DMA("gpsimd", sbf(0, 0, 2048), cst_d[:, oC:oC + 2048], [], ["s0"], "g_rope")
        DMA("gpsimd", sbf(0, 2048, 4096), cst_d[:, oS:oS + 2048], [], ["s0"], "g_rope")
        act(sm[:, 64:72], P_("snk%d" % j), AF.Exp, ["par"], ["snk"])

    def attn(j, g):
        w1k, w1b = wload("evB1_%d_%d" % (j, g))
        w2k, w2b = wload("evB2_%d_%d" % (j, g))
        RC = sbf(0, 0, 2048)
        RS = sbf(0, 2048, 4096)
        QT = sbf(1, 0, 2 * T).rearrange("p (c n) -> p c n", c=2)
        KT = sbf(2, 0, T)
        VX = sbf(2, T, 2 * T).rearrange("p (t c) -> p t c", c=128)
        SG = sbf(3, 0, 2 * T).rearrange("p (c n) -> p c n", c=2)
        VX2 = sbf(4, 4 * 512, 4 * 512 + T).rearrange("p (t c) -> p t c", c=128)
        MST = sbf(6, 0, 2 * T).rearrange("p (c n) -> p c n", c=2)
        for ti, (a, b) in enumerate(TT):
            w = b - a
            for which in range(3):
                col0 = 128 + which * 128 if which < 2 else 0
                dst = QT[:, which, a:b] if which < 2 else KT[:, a:b]
                dk = "s1" if which < 2 else "s2k"
                pk, pb = bank()
                proj(w1b, w1k, col0, 128, ti, pb, pk)
                if ti == 0:
                    act(dst, pb[:, :w], AF.Copy, [pk], [dk])
                else:
                    la = a - 256
                    qb_ = sbf(5, 4096, 4096 + w)
                    act(qb_, pb[:, :w], AF.Copy, [pk], ["s5q"])
                    pk2, pb2 = bank()
                    mm(pb2[:, :w], C_("pswap"), qb_, True, True, ["s5q", "cst"], [pk2])
                    t1 = sf(5, 0, w)
                    t2 = sf(5, 512, 512 + w)
                    tt("vector", t1, pb[:, :w], RC[:, la:la + w], ALU.mult, [pk, "s0"], ["s5a"])
                    tt("vector", t2, pb2[:, :w], RS[:, la:la + w], ALU.mult, [pk2, "s0"], ["s5b"])
                    tt("vector", dst, t1, t2, ALU.add, ["s5a", "s5b"], [dk])
            for c2 in range(2):
                pk, pb = bank()
                proj(w2b, w2k, c2 * 128, 128, ti, pb, pk)
                act(SG[:, c2, a:b], pb[:, :w], AF.Tanh, [pk], ["s3"], scale=0.5)
                stt(SG[:, c2, a:b], SG[:, c2, a:b], 1.0, pb[:, :w], ALU.add, ALU.mult, [pk, "s3"], ["s3"])
        if KSTOP == "proj":
            return
        V(lambda e: e.memset(VX[:, :, 64:128], 1.0), [], ["s2v"])
        V(lambda e: e.memset(VX2[:, :, 0:64], 1.0), [], ["s4v"])
        for t0 in range(0, 18, 8):
            nt = min(8, 18 - t0)
            pk, pb = bank()
            for t_ in range(nt):
                tk = t0 + t_
                ti = 0 if tk < 2 else 1 + (tk - 2) // 4
                for k in range(8):
                    mm(pb[:, t_ * 64:(t_ + 1) * 64], uT3[:, k, tk * 128:(tk + 1) * 128], w1b[:, k, 384:448],
                       k == 0, k == 7, [w1k, "u%d" % ti], [pk])
            src = pb[:, 0:nt * 64].rearrange("p (t c) -> p t c", c=64)
            V(lambda e, src=src, t0=t0, nt=nt: e.tensor_copy(out=VX[:, t0:t0 + nt, 0:64], in_=src), [pk], ["s2v"])
            act(VX2[:, t0:t0 + nt, 64:128], src, AF.Copy, [pk], ["s4v"])
        if KSTOP == "v":
            return
        blocks = [(0, [(0, None), (1, None)]), (128, [(0, None), (1, None)])]
        for qb in range(16):
            keys = [(0, None), (1, None)]
            if qb > 0:
                keys.append((1 + qb, None if KSTOP == "nomask" else "mprev"))
            keys.append((2 + qb, None))
            if qb < 15:
                keys.append((3 + qb, None if KSTOP == "nomask" else "mnext"))
            blocks.append((256 + qb * 128, keys))
        snk = sm[:, 64 + 2 * g:64 + 2 * g + 2]
        sk3 = snk.unsqueeze(2).to_broadcast([128, 2, 128])
        items = []
        for bi, (q0, keys) in enumerate(blocks):
            for idx, (kt, mt) in enumerate(keys):
                items.append((bi, q0, idx, len(keys), kt, mt))
        sc_ctr = [0]

        def score_bank():
            i = sc_ctr[0] % 4
            sc_ctr[0] += 1
            return "ps%d" % i, ps[i]

        def emit_scores(n):
            bi, q0, idx, nk, kt, mt = items[n]
            pt = sbf(4, (n % 4) * 512, (n % 4 + 1) * 512)
            ptk = "s4p%d" % (n % 4)
            for half in range(2):
                psk, ps_ = score_bank()
                if mt is not None:
                    mm(ps_[:, 0:256], ident, C_(mt, 0, 256), True, False, ["cst"], [psk])
                mm(ps_[:, 0:256].rearrange("p (c n) -> p c n", c=2),
                   KT[half * 64:(half + 1) * 64, kt * 128:(kt + 1) * 128],
                   QT[half * 64:(half + 1) * 64, :, q0:q0 + 128], mt is None, True, ["s2k", "s1"], [psk])
                act(pt[:, half * 256:(half + 1) * 256], ps_[:, 0:256], AF.Exp, [psk], [ptk], scale=0.125)

        def emit_pv(n):
            bi, q0, idx, nk, kt, mt = items[n]
            pt = sbf(4, (n % 4) * 512, (n % 4 + 1) * 512)
            ptk = "s4p%d" % (n % 4)
            ia, ib = (4, 5) if bi % 2 == 0 else (6, 7)
            pek, pe_ = "ps%d" % ia, ps[ia]
            pok, po_ = "ps%d" % ib, ps[ib]
            mm(pe_[:, 0:256], VX[:, kt, :], pt[:, 0:256], idx == 0, idx == nk - 1, ["s2v", ptk], [pek])
            mm(po_[:, 0:256], VX2[:, kt, :], pt[:, 256:512], idx == 0, idx == nk - 1, ["s4v", ptk], [pok])
            if idx != nk - 1:
                return
            q = bi % 2
            rd = sf(5, 1024 + q * 512, 1280 + q * 512)
            t1 = sf(5, 1280 + q * 512, 1536 + q * 512)
            rk, tk = "s5r%d" % q, "s5t%d" % q
            tt("vector", rd[64:128, :].rearrange("p (c n) -> p c n", c=2),
               pe_[64:128, 0:256].rearrange("p (c n) -> p c n", c=2), sk3[64:128], ALU.add, [pek, "snk"], [rk])
            tt("vector", rd[0:64, :].rearrange("p (c n) -> p c n", c=2),
               po_[0:64, 0:256].rearrange("p (c n) -> p c n", c=2), sk3[0:64], ALU.add, [pok, "snk"], [rk])
            V(lambda e, rd=rd: e.reciprocal(out=rd, in_=rd), [rk], [rk])
            tt("vector", t1[0:64, :], pe_[0:64, 0:256], rd[64:128, :], ALU.mult, [pek, rk], [tk])
            tt("vector", t1[64:128, :], po_[64:128, 0:256], rd[0:64, :], ALU.mult, [pok, rk], [tk])
            stt(MST[:, :, q0:q0 + 128], t1.rearrange("p (c n) -> p c n", c=2), 0.5, SG[:, :, q0:q0 + 128],
                ALU.mult, ALU.mult, [tk, "s3"], ["s6"])

        for n in range(len(items) + 1):
            if n < len(items):
                emit_scores(n)
            if n >= 1:
                emit_pv(n - 1)
        for c2 in range(2):
            DMA("sync", mixD[8 + 2 * g + c2], MST[:, c2, :], ["s6"], ["mixD"], "mixst")

    def layer_begin(l):
        ada_pump(100)
        if l + 1 < nlayers:
            pending_ada.extend(ada_steps(l + 1))
        return ada_get(l)

    def even_layer(l):
        j = l // 2
        m3, mk = layer_begin(l)
        S.fence()
        norm(l, m3, mk)
        S.fence()
        rg_setup(j)
        for i in range(4):
            wk, wb = wload("evA%d_%d" % (j, i))
            for hh in range(2):
                rglru(j, 2 * i + hh, wk, wb, hh * 256, hh * 256 + 128)
            ada_pump()
        S.fence()
        outproj(["evO%d_0_0" % j, "evO%d_0_1" % j], 0, m3, mk)
        S.fence()
        if os.environ.get("KSKIP") == "attn":
            return
        attn_setup(j)
        for g in range(1 if KSTOP else 4):
            attn(j, g)
            ada_pump()
        S.fence()
        if KSTOP:
            return
        outproj(["evO%d_1_0" % j, "evO%d_1_1" % j], 8, m3, mk)
        S.fence()

    CH = 64
    NCH = T // CH

    def odd_head(j, hd, m3, mk):
        w1k, w1b = wload("od1_%d_%d" % (j, hd))
        w2k, w2b = wload("od2_%d_%d" % (j, hd))
        qs = sf(0)
        vtok = sbf(1, 0, NCH * 128).rearrange("p (t c) -> p t c", c=128)
        vtile = vtile_t[:, :]
        fbuf = sf(2)
        one_b = P_("one").to_broadcast([128, T])
        lb = sm[:, 80 + hd:81 + hd]
        oml = sm[:, 88 + hd:89 + hd]
        for ti, (a, b) in enumerate(TT):
            w = b - a
            pk, pb = bank()
            proj(w1b, w1k, 384, 128, ti, pb, pk)
            act(qs[:, a:b], pb[:, :w], AF.Tanh, [pk], ["s0"], scale=0.5)
            stt(qs[:, a:b], qs[:, a:b], 1.0, pb[:, :w], ALU.add, ALU.mult, [pk, "s0"], ["s0"])
            pk, pb = bank()
            proj(w1b, w1k, 256, 128, ti, pb, pk)
            act(vtile[:, :w], pb[:, :w], AF.Copy, [pk], ["vtile"])
            pk2, pb2 = bank()
            pbT = pb2[:, :].bitcast(BF16)
            nb = w // CH
            for i in range(nb):
                PE(lambda e, i=i, pbT=pbT: e.transpose(pbT[0:CH, i * 128:(i + 1) * 128], vtile[:, i * CH:(i + 1) * CH], ident),
                   ["vtile", "cst"], [pk2])
            c0 = a // CH
            V(lambda e, pbT=pbT, c0=c0, nb=nb: e.tensor_copy(
                out=vtok[0:CH, c0:c0 + nb, :], in_=pbT[0:CH, 0:nb * 128].rearrange("p (t c) -> p t c", c=128)),
              [pk2], ["s1v"])
        qk = {}
        for d in range(2):
            gbuf, Bg, Em = sf(3), sf(4), sf(5)
            sv = 128 + d * 256
            Bst, Bmid, tA = sm[:, sv:sv + NCH], sm[:, sv + NCH:sv + 2 * NCH], sm[:, sv + 2 * NCH:sv + 3 * NCH]
            e1, esn, e2n = (sm[:, sv + 3 * NCH:sv + 4 * NCH], sm[:, sv + 4 * NCH:sv + 5 * NCH],
                            sm[:, sv + 5 * NCH:sv + 6 * NCH])
            svk = "smo%d" % d
            for ti, (a, b) in enumerate(TT):
                pk, pb = bank()
                proj(w1b, w1k, d * 128, 128, ti, pb, pk)
                act(fbuf[:, a:b], pb[:, :b - a], AF.Tanh, [pk], ["s2"], scale=0.5)
            ts("vector", fbuf[:, 0:T], fbuf[:, 0:T], oml, lb, ALU.mult, ALU.add, ["s2", "lbv"], ["s2"])
            act(gbuf[:, 0:T], fbuf[:, 0:T], AF.Ln, ["s2"], ["s3"])
            if d == 0:
                V(lambda e, Bg=Bg, gbuf=gbuf: e.tensor_tensor_scan(
                    out=Bg[:, 0:T], data0=one_b, data1=gbuf[:, 0:T], initial=0.0, op0=ALU.mult, op1=ALU.add),
                  ["s3", "par"], ["s4"])
                Bend = Bg[:, CH - 1:T:CH]
            else:
                V(lambda e, Bg=Bg, gbuf=gbuf: e.tensor_tensor_scan(
                    out=Bg[:, T - 1::-1], data0=one_b, data1=gbuf[:, T - 1::-1], initial=0.0,
                    op0=ALU.mult, op1=ALU.add), ["s3", "par"], ["s4"])
                Bend = Bg[:, 0:T:CH]
            V(lambda e, Bmid=Bmid, Bg=Bg: e.tensor_copy(out=Bmid, in_=Bg[:, CH // 2:T:CH]), ["s4"], [svk])
            if d == 0:
                V(lambda e, Bst=Bst: e.memset(Bst[:, 0:1], 0.0), [], [svk])
                V(lambda e, Bst=Bst, Bend=Bend: e.tensor_copy(out=Bst[:, 1:NCH], in_=Bend[:, 0:NCH - 1]), ["s4"], [svk])
            else:
                V(lambda e, Bst=Bst: e.memset(Bst[:, NCH - 1:NCH], 0.0), [], [svk])
                V(lambda e, Bst=Bst, Bend=Bend: e.tensor_copy(out=Bst[:, 0:NCH - 1], in_=Bend[:, 1:NCH]), ["s4"], [svk])
            tt("vector", tA, Bend, Bst, ALU.subtract, ["s4", svk], [svk + "t"])
            act(e1, tA, AF.Exp, [svk + "t"], [svk + "e"])
            tt("vector", tA, Bmid, Bst, ALU.subtract, [svk, svk + "e"], [svk + "t"])
            act(esn, tA, AF.Exp, [svk + "t"], [svk + "e"])
            tt("vector", tA, Bend, Bmid, ALU.subtract, ["s4", svk, svk + "e"], [svk + "t"])
            act(e2n, tA, AF.Exp, [svk + "t"], [svk + "e"])
            ts("vector", e2n, e2n, -1.0, None, ALU.mult, None, [svk + "e"], [svk + "e"])
            ts("vector", esn, esn, -1.0, None, ALU.mult, None, [svk + "e"], [svk + "e"])
            d3 = gbuf[:, 0:T].rearrange("p (c n) -> p c n", n=CH)
            tt("vector", d3, Bg[:, 0:T].rearrange("p (c n) -> p c n", n=CH),
               Bmid.unsqueeze(2).to_broadcast([128, NCH, CH]), ALU.subtract, ["s4", svk], ["s3"])
            act(Bg[:, 0:T], gbuf[:, 0:T], AF.Exp, ["s3"], ["s4"])
            act(Em[:, 0:T], gbuf[:, 0:T], AF.Exp, ["s3"], ["s5"], scale=-1.0)
            ds = 6 if d == 0 else 3
            qt = sbf(ds, 0, T)
            kt = sbf(ds, T, 2 * T)
            dk = "s%d" % ds
            stt(qt, qs[:, 0:T], -0.5, Bg[:, 0:T], ALU.mult, ALU.mult, ["s0", "s4"], [dk])
            stt(kt, fbuf[:, 0:T], 1.0, Em[:, 0:T], ALU.subtract, ALU.mult, ["s2", "s5"], [dk])
            qk[d] = (qt, kt, dk, e1, esn, e2n, svk + "e")
        osum = sf(0)
        okall = ["s0o%d" % c for c in range(NCH)]
        S.add("vector", lambda e: e.memset(dummy[:, :], 0.0), [], ["s0"] + okall)
        for x_ in range(6):
            V(lambda e, x_=x_: e.memset(atm_t[x_][:, :], 0.0), [], ["atm%d" % x_])
        nctx = 256 // CH
        order = {0: list(range(NCH)), 1: list(range(nctx - 1, -1, -1)) + list(range(NCH - 1, nctx - 1, -1))}
        step_of = {d: {c: i for i, c in enumerate(order[d])} for d in range(2)}
        ob_ctr = [0]

        def obank():
            i = ob_ctr[0] % 8
            ob_ctr[0] += 1
            return "ps%d" % i, ps[i]

        Hh = CH // 2

        def part1(i, d):
            qt, kt, dk, e1, esn, e2n, sek = qk[d]
            c = order[d][i]
            c0_, cm_, c1_ = c * CH, c * CH + Hh, (c + 1) * CH
            x_ = d * 3 + i % 3
            pka, pa = obank()
            if d == 0:
                mm(pa[0:CH, Hh:CH], kt[:, c0_:c1_], qt[:, cm_:c1_], True, True, [dk], [pka])
                mm(pa[0:Hh, 0:Hh], kt[:, c0_:cm_], qt[:, c0_:cm_], True, True, [dk], [pka])
                mm(pa[Hh:CH, 0:Hh], kt[:, cm_:c1_], zb[:, 0:Hh], True, True, [dk, "zb"], [pka])
            else:
                mm(pa[0:CH, 0:Hh], kt[:, c0_:c1_], qt[:, c0_:cm_], True, True, [dk], [pka])
                mm(pa[Hh:CH, Hh:CH], kt[:, cm_:c1_], qt[:, cm_:c1_], True, True, [dk], [pka])
                mm(pa[0:Hh, Hh:CH], kt[:, c0_:cm_], zb[:, 0:Hh], True, True, [dk, "zb"], [pka])
            am = atm_t[x_][:, :]
            mo, mw = CL.off["maskF" if d == 0 else "maskB"]
            msk = cstu[0:CH, mo:mo + CH]
            V(lambda e, am=am, msk=msk, pa=pa: e.copy_predicated(out=am, mask=msk, data=pa[0:CH, 0:CH]),
              [pka, "cst"], ["atm%d" % x_])
            pkt, pt_ = obank()
            ptT = pt_[:, :].bitcast(BF16)
            PE(lambda e, ptT=ptT, kt=kt, c0_=c0_, c1_=c1_: e.transpose(ptT[0:CH, 0:128], kt[:, c0_:c1_], ident),
               [dk, "cst"], [pkt])
            act(ktok_t[x_][:, :], ptT[0:CH, 0:128], AF.Copy, [pkt], ["ktok%d" % x_])

        def part2(i, d):
            qt, kt, dk, e1, esn, e2n, sek = qk[d]
            c = order[d][i]
            cs = slice(c * CH, (c + 1) * CH)
            first, last = i == 0, i == NCH - 1
            x_ = d * 3 + i % 3
            am, amk = atm_t[x_][:, :], "atm%d" % x_
            kk, kkk = ktok_t[x_][:, :], "ktok%d" % x_
            pko, po = obank()
            if not first:
                mm(po[:, 0:CH], Sbf[d][:, :], qt[:, cs], True, False, ["Sbf%d" % d, dk], [pko])
            mm(po[:, 0:CH], vtok[0:CH, c, :], am, first, True, ["s1v", amk], [pko])
            ok = "s0o%d" % c
            if step_of[1 - d][c] > i or (step_of[1 - d][c] == i and d == 0):
                act(osum[:, cs], po[:, 0:CH], AF.Copy, [pko], [ok])
            else:
                tt("vector", osum[:, cs], po[:, 0:CH], osum[:, cs], ALU.add, [pko, ok], [ok])
            pku, pu = obank()
            mm(pu[:, 0:128], kk, vtok[0:CH, c, :], True, True, [kkk, "s1v"], [pku])
            sk = "Sst%d" % d
            if first:
                ts("vector", Sst[d][:, :], pu[:, 0:128], e2n[:, c:c + 1], None, ALU.mult, None, [pku, sek], [sk])
            else:
                act(Stm[d][:, :], Sst[d][:, :], AF.Copy, [sk, sek], ["Stm%d" % d], scale=e1[:, c:c + 1])
                stt(Sst[d][:, :], pu[:, 0:128], e2n[:, c:c + 1], Stm[d][:, :], ALU.mult, ALU.add,
                    [pku, sek, "Stm%d" % d], [sk])
            if not last:
                cn = order[d][i + 1]
                act(Sbf[d][:, :], Sst[d][:, :], AF.Copy, [sk, sek], ["Sbf%d" % d], scale=esn[:, cn:cn + 1])

        flat = [(i, d) for i in range(NCH) for d in range(2)]
        LOOK = 4
        for n in range(len(flat) + LOOK):
            if n < len(flat):
                part1(*flat[n])
            if n >= LOOK:
                part2(*flat[n - LOOK])
        okeys = okall
        S.fence()
        mst = sbf(2, SLOTW, SLOTW + T)
        gn = P_("gn%d" % j)
        for ti, (a, b) in enumerate(TT):
            w = b - a
            q = ti % 2
            sq = sbf(2, q * 512, q * 512 + w)
            sqk = "s2q%d" % q
            act(sq, osum[:, a:b], AF.Square, okeys, [sqk])
            pk, pb = bank()
            mm(pb[:, :w], ones_bf, sq, True, True, [sqk, "cst"], [pk])
            rs = sf(6, q * 512, q * 512 + w)
            rk = "s6r%d" % q
            ts("vector", rs, pb[:, :w], 1.0 / 128, 1e-6, ALU.mult, ALU.add, [pk], [rk])
            act(rs, rs, AF.Sqrt, [rk], [rk])
            V(lambda e, rs=rs: e.reciprocal(out=rs, in_=rs), [rk], [rk])
            stt(osum[:, a:b], osum[:, a:b], gn[:, 0:1], rs, ALU.mult, ALU.mult, okeys + ["par", rk], ["s0y"])
        for ti, (a, b) in enumerate(TT):
            w = b - a
            q = ti % 2
            pk, pb = bank()
            proj(w2b, w2k, 0, 128, ti, pb, pk)
            sg = sf(6, 1024 + q * 512, 1024 + q * 512 + w)
            sgk = "s6g%d" % q
            act(sg, pb[:, :w], AF.Tanh, [pk], [sgk], scale=0.5)
            stt(sg, sg, 1.0, pb[:, :w], ALU.add, ALU.mult, [pk, sgk], [sgk])
            stt(mst[:, a:b], sg, 0.5, osum[:, a:b], ALU.mult, ALU.mult, [sgk, "s0y"], ["s2m"])
        DMA("sync", mixD[hd], mst, ["s2m"], ["mixD"], "mixst")

    def odd_layer(l):
        j = l // 2
        m3, mk = layer_begin(l)
        S.fence()
        norm(l, m3, mk)
        S.fence()
        if j == 0:
            V(lambda e: e.memset(sm[:, 80:88], 0.5), [], ["lbv"])
            V(lambda e: e.memset(sm[:, 88:96], 0.5), [], ["lbv"])
        else:
            tt("vector", sm[:, 96:104], P_("lbr1"), P_("lbr0"), ALU.subtract, ["par"], ["lbt"])
            act(sm[:, 96:104], sm[:, 96:104], AF.Sigmoid, ["lbt"], ["lbt"])
            ts("vector", sm[:, 88:96], sm[:, 96:104], -0.5, 0.5, ALU.mult, ALU.add, ["lbt"], ["lbv"])
            tt("vector", sm[:, 80:88], sm[:, 96:104], sm[:, 88:96], ALU.add, ["lbt", "lbv"], ["lbv"])
        for hd in range(8):
            odd_head(j, hd, m3, mk)
            S.fence()
            ada_pump()
        outproj(["odO%d_0" % j, "odO%d_1" % j], 0, m3, mk)
        S.fence()

    pending_ada.extend(ada_steps(0))
    for l in range(nlayers):
        if l % 2 == 0:
            even_layer(l)
        else:
            odd_layer(l)

    S.fence()
    for ti in range(1, 5):
        a, b = TT[ti]
        w = b - a
        sq = sbf(ti % 2, 0, 8 * 512).rearrange("p (k n) -> p k n", k=8)
        sqk = "s%d" % (ti % 2)
        act(sq[:, :, :w], hT3[:, :, a:b], AF.Square, ["h%d" % ti], [sqk])
        pk, pb = bank()
        for k in range(8):
            mm(pb[:, :w], ones_bf, sq[:, k, :w], k == 0, k == 7, [sqk, "cst"], [pk])
        rs = sf(2, (ti % 2) * 512, (ti % 2) * 512 + w)
        rk = "s2r%d" % (ti % 2)
        ts("vector", rs, pb[:, :w], 1.0 / 1024, 1e-6, ALU.mult, ALU.add, [pk], [rk])
        act(rs, rs, AF.Sqrt, [rk], [rk])
        V(lambda e, rs=rs: e.reciprocal(out=rs, in_=rs), [rk], [rk])
        fn = P_("fnw")
        for k in range(8):
            stt(hT3[:, k, a:b], hT3[:, k, a:b], fn[:, k:k + 1], rs, ALU.mult, ALU.mult,
                ["h%d" % ti, "par", rk], ["h%d" % ti])
        for k in range(8):
            DMA("sync", out_d[:, k * 2048 + a - 256:k * 2048 + b - 256], hT3[:, k, a:b], ["h%d" % ti], [],
                "out%d" % k)
    S.emit()
    return nc


_CACHE = {}


def kernel(**inp):
    inp = {k: np.asarray(v) for k, v in inp.items()}
    WL = build_weights(inp)
    CL = build_consts()
    wall = WL.build()
    cst = CL.build()
    in_maps = []
    PL = None
    for b in range(8):
        PL = build_params(inp, b)
        h0 = np.concatenate([inp["ctx"][b], inp["x"][b]], axis=0).astype(np.float32)
        h0 = np.ascontiguousarray(h0.T.reshape(8, 128, T).transpose(1, 0, 2).reshape(128, 8 * T))
        in_maps.append({"h0": h0, "par": PL.build(), "wall": wall, "cst": cst})
    nc = build_program(WL, PL, CL, NLAYERS)
    res = run_bass_kernel_spmd(nc, in_maps, core_ids=list(range(8)))
    out = np.empty((8, 2048, 1024), np.float32)
    for b in range(8):
        o = np.asarray(res.results[b]["outT"]).reshape(128, 8, 2048)
        out[b] = o.transpose(2, 1, 0).reshape(2048, 1024)
    return out
```

```python
import os
import numpy as np
import concourse.bass as bass
import concourse.mybir as mybir
from concourse.bass_utils import run_bass_kernel_spmd

F32 = mybir.dt.float32
BF16 = mybir.dt.bfloat16
AF = mybir.ActivationFunctionType
ALU = mybir.AluOpType

ENGS = ("tensor", "vector", "scalar", "gpsimd", "sync")
NLAYERS = 4
KSTOP = os.environ.get('KSTOP', '')
T = 2304
TT = [(0, 256), (256, 768), (768, 1280), (1280, 1792), (1792, 2304)]
SLOTW = 2320


class Op:
    __slots__ = ("eng", "fn", "deps", "needed", "lane", "val", "pos")

    def __init__(self, eng, fn, lane):
        self.pos = -1
        self.eng = eng
        self.fn = fn
        self.deps = []
        self.needed = False
        self.lane = lane
        self.val = None


class Sched:
    def __init__(self, nc):
        self.nc = nc
        self.q = {e: [] for e in ENGS}
        self.last_w = {}
        self.readers = {}
        self.lane_cnt = {}
        self.epoch = None

    @staticmethod
    def _is_slot(k):
        return len(k) > 1 and k[0] == "s" and k[1].isdigit()

    def fence(self, eng="sync"):
        op = Op(eng, lambda e: e.nop(), None)
        deps = {}
        for k in list(self.last_w.keys()):
            if self._is_slot(k):
                w = self.last_w.pop(k)
                deps[id(w)] = w
                for r in self.readers.pop(k, ()):
                    deps[id(r)] = r
        if self.epoch is not None:
            deps[id(self.epoch)] = self.epoch
        for d in self._prune(deps.values()):
            op.deps.append(d)
            d.needed = True
        op.pos = len(self.q[eng])
        self.q[eng].append(op)
        self.epoch = op
        return op

    @staticmethod
    def _prune(deps):
        best = {}
        out = []
        for d in deps:
            if d.lane is not None:
                out.append(d)
            elif d.eng not in best or d.pos > best[d.eng].pos:
                best[d.eng] = d
        return out + list(best.values())

    def add(self, eng, fn, reads=(), writes=(), lane=None):
        op = Op(eng, fn, lane)
        deps = {}
        if self.epoch is not None:
            for k in list(reads) + list(writes):
                if self._is_slot(k) and k not in self.last_w:
                    self.last_w[k] = self.epoch
        for k in reads:
            w = self.last_w.get(k)
            if w is not None:
                deps[id(w)] = w
            if k.startswith("ps"):
                for r in self.readers.get(k, ()):
                    if r.eng != eng:
                        deps[id(r)] = r
        for k in writes:
            w = self.last_w.get(k)
            if w is not None:
                deps[id(w)] = w
            for r in self.readers.get(k, ()):
                deps[id(r)] = r
        cand = [d for d in deps.values()
                if d is not op and not (d.eng == "tensor" and eng == "tensor" and d.lane is None)]
        for d in self._prune(cand):
            op.deps.append(d)
            d.needed = True
        for k in reads:
            self.readers.setdefault(k, []).append(op)
        for k in writes:
            self.last_w[k] = op
            self.readers[k] = []
        op.pos = len(self.q[eng])
        self.q[eng].append(op)
        if lane is not None:
            c = self.lane_cnt.get(lane, 0) + 1
            self.lane_cnt[lane] = c
            op.val = 16 * c
        return op

    def emit(self):
        nc = self.nc
        for e in ENGS:
            c = 0
            for op in self.q[e]:
                if op.lane is None and op.needed:
                    c += 1
                    op.val = c
        lanes = list(self.lane_cnt.keys())
        from contextlib import ExitStack
        with ExitStack() as st:
            esem = {e: st.enter_context(nc.semaphore("s_" + e)) for e in ENGS}
            lsem = {l: st.enter_context(nc.semaphore("l_%d" % i)) for i, l in enumerate(lanes)}
            block = st.enter_context(nc.Block())

            def semof(op):
                return lsem[op.lane] if op.lane is not None else esem[op.eng]

            def run(engname):
                def body(e):
                    known = {}
                    for op in self.q[engname]:
                        need = {}
                        for d in op.deps:
                            s = semof(d)
                            if d.val > need.get(s.num, (None, 0))[1]:
                                need[s.num] = (s, d.val)
                        for key, (s, v) in need.items():
                            if known.get(key, 0) >= v:
                                continue
                            e.wait_ge(s, v)
                            known[key] = v
                        ins = op.fn(e)
                        if op.lane is not None:
                            ins.then_inc(lsem[op.lane], 16)
                        elif op.needed:
                            ins.then_inc(esem[engname], 1)
                    if engname == "sync":
                        for l in lanes:
                            e.wait_ge(lsem[l], 16 * self.lane_cnt[l])
                        for en in ENGS:
                            tot = sum(1 for o in self.q[en] if o.lane is None and o.needed)
                            if tot:
                                e.wait_ge(esem[en], tot)
                return body

            block.tensor(run("tensor"))
            block.vector(run("vector"))
            block.scalar(run("scalar"))
            block.gpsimd(run("gpsimd"))
            block.sync(run("sync"))


def fm(v):
    v = np.asarray(v, np.float32)
    return np.ascontiguousarray(v.reshape(-1, 128).T)


def wtile(W):
    W = np.asarray(W, np.float32)
    w = W.shape[1]
    return W.reshape(8, 128, w).transpose(1, 0, 2).reshape(128, 8 * w)


PERM = np.concatenate([np.arange(0, 64, 2), np.arange(1, 64, 2)])


class Layout:
    def __init__(self):
        self.off = {}
        self.parts = []
        self.n = 0

    def put(self, name, arr):
        arr = np.ascontiguousarray(arr, np.float32)
        assert arr.shape[0] == 128
        self.off[name] = (self.n, arr.shape[1])
        self.parts.append(arr)
        self.n += arr.shape[1]

    def build(self):
        return np.ascontiguousarray(np.concatenate(self.parts, axis=1))


def build_weights(inp):
    L = Layout()
    for l in range(4):
        for t in range(6):
            L.put("ada%d_%d" % (l, t), wtile(inp["ada_w"][l][:, t * 512:(t + 1) * 512]))
    NS = 1536
    for j in range(2):
        W = inp["ev_w_in"][j]
        for i in range(4):
            cols = []
            for h in (2 * i, 2 * i + 1):
                cols += list(range(h * 128, h * 128 + 128))
                cols += list(range(NS + h * 128, NS + h * 128 + 128))
            L.put("evA%d_%d" % (j, i), wtile(W[:, cols]))
        for g in range(4):
            kc = list(1024 + g * 64 + PERM)
            qc = []
            for hh in range(4 * g, 4 * g + 4):
                qc += list(NS + 1024 + hh * 64 + PERM)
            vc = list(range(1024 + 256 + g * 64, 1024 + 256 + g * 64 + 64))
            L.put("evB1_%d_%d" % (j, g), wtile(W[:, kc + kc + qc + vc]))
            gc = list(range(NS + 2048 + g * 256, NS + 2048 + g * 256 + 256))
            L.put("evB2_%d_%d" % (j, g), wtile(W[:, gc]))
        Wo = inp["ev_w_out"][j]
        for half in range(2):
            for t in range(2):
                L.put("evO%d_%d_%d" % (j, half, t),
                      wtile(Wo[half * 1024:(half + 1) * 1024, t * 512:(t + 1) * 512]))
        for h in range(8):
            m = np.stack([inp["ev_rg_wx"][j][0][h], inp["ev_rg_wa"][j][0][h],
                          inp["ev_rg_wx"][j][1][h], inp["ev_rg_wa"][j][1][h]], axis=1)
            L.put("rg%d_%d" % (j, h), m.reshape(128, 512))
    for j in range(2):
        W = inp["od_w_in"][j]
        for hd in range(8):
            cols = []
            for base in (0, 1024, 2048, 3072):
                cols += list(range(base + hd * 128, base + hd * 128 + 128))
            L.put("od1_%d_%d" % (j, hd), wtile(W[:, cols]))
            L.put("od2_%d_%d" % (j, hd), wtile(W[:, 4096 + hd * 128:4096 + hd * 128 + 128]))
        Wo = inp["od_w_out"][j]
        for t in range(2):
            L.put("odO%d_%d" % (j, t), wtile(Wo[:, t * 512:(t + 1) * 512]))
    return L


def build_params(inp, b, L=None):
    L = Layout()
    L.put("c", fm(inp["c"][b]))
    L.put("cc", fm(inp["c_ctx"]))
    for l in range(4):
        L.put("nw%d" % l, fm(inp["norm_w"][l]))
        L.put("ab%d" % l, fm(inp["ada_b"][l]))
    L.put("fnw", fm(inp["final_norm_w"]))
    for j in range(2):
        cw = np.concatenate([fm(inp["ev_conv_w"][j][k]) for k in range(4)], axis=1)
        L.put("cw%d" % j, cw)
        L.put("cb%d" % j, fm(inp["ev_conv_b"][j]))
        for d in range(2):
            L.put("bx%d%d" % (j, d), fm(inp["ev_rg_bx"][j][d]))
            L.put("ba%d%d" % (j, d), fm(inp["ev_rg_ba"][j][d]))
            L.put("lam%d%d" % (j, d), fm(inp["ev_rg_lambda"][j][d]))
        sk = np.zeros((128, 8), np.float32)
        for g in range(4):
            for c2 in range(2):
                sk[64:, g * 2 + c2] = inp["ev_sink"][j][4 * g + 2 * c2]
                sk[:64, g * 2 + c2] = inp["ev_sink"][j][4 * g + 2 * c2 + 1]
        L.put("snk%d" % j, sk)
    L.put("lbr0", fm(inp["od_lb_raw"][0]))
    L.put("lbr1", fm(inp["od_lb_raw"][1]))
    for j in range(2):
        L.put("gn%d" % j, np.asarray(inp["od_gnorm_w"][j], np.float32).reshape(128, 1))
    L.put("one", np.ones((128, 1), np.float32))
    L.put("zero", np.zeros((128, 1), np.float32))
    return L


def build_consts():
    L = Layout()
    L.put("ident", np.eye(128, dtype=np.float32))
    P = np.zeros((128, 128), np.float32)
    for p in range(128):
        q = (p // 64) * 64 + ((p % 64) + 32) % 64
        P[p, q] = 1.0
    L.put("pswap", P)
    L.put("ones", np.ones((128, 128), np.float32))
    jj = np.arange(128)[:, None]
    ii = np.arange(128)[None, :]
    mprev = np.where(jj >= ii, 0.0, -30000.0).astype(np.float32)
    mnext = np.where(jj <= ii, 0.0, -30000.0).astype(np.float32)
    L.put("mprev", np.tile(mprev, (1, 4)))
    L.put("mnext", np.tile(mnext, (1, 4)))
    L.put("maskF", (jj <= ii).astype(np.float32))
    L.put("maskB", (jj >= ii).astype(np.float32))
    n = np.arange(2048)
    row = (n // 64).astype(np.float32)
    col = (n % 64).astype(np.float32)
    inv = (10000.0 ** (-np.arange(0, 32, 2, dtype=np.float32) / 32)).astype(np.float32)
    ang = np.concatenate([row[:, None] * inv, col[:, None] * inv], axis=-1)
    cos = np.cos(ang).astype(np.float32).T
    sin = np.sin(ang).astype(np.float32).T
    C = np.concatenate([cos, cos, cos, cos], axis=0)
    Sg = np.concatenate([-sin, sin, -sin, sin], axis=0)
    L.put("ropeC", C)
    L.put("ropeS", Sg)
    return L


def build_program(WL, PL, CL, nlayers):
    nc = bass.Bass("TRN2", target_bir_lowering=False)
    S = Sched(nc)
    h0_d = nc.dram_tensor("h0", [128, 8 * T], F32, kind="ExternalInput").ap()
    par_d = nc.dram_tensor("par", [128, PL.n], F32, kind="ExternalInput").ap()
    wall_d = nc.dram_tensor("wall", [128, WL.n], F32, kind="ExternalInput").ap()
    cst_d = nc.dram_tensor("cst", [128, CL.n], F32, kind="ExternalInput").ap()
    out_d = nc.dram_tensor("outT", [128, 8 * 2048], F32, kind="ExternalOutput").ap()
    mixD = nc.dram_tensor("mixD", [16, 128, T], BF16).ap()

    hT = nc.alloc_sbuf_tensor("hT", [128, 8 * T], F32)
    uT = nc.alloc_sbuf_tensor("uT", [128, 8 * T], BF16)
    slots = [nc.alloc_sbuf_tensor("slot%d" % i, [128, SLOTW], F32) for i in range(9)]
    par = nc.alloc_sbuf_tensor("par_sb", [128, PL.n], F32)
    NCB = CL.off["ropeC"][0]
    cstb = nc.alloc_sbuf_tensor("cst_sb", [128, NCB], BF16)
    modT = [nc.alloc_sbuf_tensor("modT%d" % i, [128, 48], F32) for i in range(4)]
    w1vs = [nc.alloc_sbuf_tensor("w1v%d" % i, [128, 16], F32) for i in range(4)]
    scT = nc.alloc_sbuf_tensor("scT", [128, 16], BF16)
    sm = nc.alloc_sbuf_tensor("small", [128, 768], F32)
    vtile_t = nc.alloc_sbuf_tensor("vtile", [128, 512], BF16)
    ktok_t = [nc.alloc_sbuf_tensor("ktok%d" % i, [64, 128], BF16) for i in range(6)]
    atm_t = [nc.alloc_sbuf_tensor("atm%d" % i, [64, 64], BF16) for i in range(6)]
    cstu = cstb[:, :].bitcast(mybir.dt.uint16)
    zb = nc.alloc_sbuf_tensor("zb", [128, 32], BF16)
    dummy = nc.alloc_sbuf_tensor("mkdummy", [128, 2], F32)
    rgw = [nc.alloc_sbuf_tensor("rgw%d" % i, [128, 512], BF16) for i in range(2)]
    Sst = [nc.alloc_sbuf_tensor("Sst%d" % i, [128, 128], F32) for i in range(2)]
    Stm = [nc.alloc_sbuf_tensor("Stm%d" % i, [128, 128], F32) for i in range(2)]
    Sbf = [nc.alloc_sbuf_tensor("Sbf%d" % i, [128, 128], BF16) for i in range(2)]
    ps = [nc.alloc_psum_tensor("ps%d" % i, [128, 512], F32) for i in range(8)]

    hT3 = hT[:, :].rearrange("p (k n) -> p k n", k=8)
    uT3 = uT[:, :].rearrange("p (k n) -> p k n", k=8)

    def P_(name, a=0, b=None):
        o, w = PL.off[name]
        b = w if b is None else b
        return par[:, o + a:o + b]

    def C_(name, a=0, b=None):
        o, w = CL.off[name]
        b = w if b is None else b
        return cstb[:, o + a:o + b]

    def sbf(i, a=0, b=2 * SLOTW):
        return slots[i][:, :].bitcast(BF16)[:, a:b]

    def sf(i, a=0, b=SLOTW):
        return slots[i][:, a:b]

    bank_ctr = [0]

    def bank():
        i = bank_ctr[0] % 6
        bank_ctr[0] += 1
        return "ps%d" % i, ps[i]

    def PE(fn, r, w):
        return S.add("tensor", fn, r, w)

    def V(fn, r, w):
        return S.add("vector", fn, r, w)

    def A(fn, r, w):
        return S.add("scalar", fn, r, w)

    def G(fn, r, w):
        return S.add("gpsimd", fn, r, w)

    def DMA(eng, out, in_, r, w, lane):
        return S.add(eng, lambda e: e.dma_start(out=out, in_=in_), r, w, lane=lane)

    def mm(out, lhsT, rhs, start, stop, r, w):
        return PE(lambda e: e.matmul(out, lhsT=lhsT, rhs=rhs, start=start, stop=stop), r, w)

    def act(out, in_, func, r, w, bias=None, scale=None):
        kw = {}
        if bias is not None:
            kw["bias"] = bias
        if scale is not None:
            kw["scale"] = scale
        return A(lambda e: e.activation(out=out, in_=in_, func=func, **kw), r, w)

    def tt(eng, out, in0, in1, op, r, w):
        return S.add(eng, lambda e: e.tensor_tensor(out=out, in0=in0, in1=in1, op=op), r, w)

    def ts(eng, out, in0, s1, s2, op0, op1, r, w):
        if op1 is None:
            return S.add(eng, lambda e: e.tensor_scalar(out=out, in0=in0, scalar1=s1, scalar2=None, op0=op0), r, w)
        return S.add(eng, lambda e: e.tensor_scalar(out=out, in0=in0, scalar1=s1, scalar2=s2, op0=op0, op1=op1), r, w)

    def stt(out, in0, scalar, in1, op0, op1, r, w):
        return V(lambda e: e.scalar_tensor_tensor(out=out, in0=in0, scalar=scalar, in1=in1, op0=op0, op1=op1), r, w)

    wctr = [0]

    def wload(name):
        o, n = WL.off[name]
        i = wctr[0] % 2
        wctr[0] += 1
        key = "w%d" % i
        dst = sbf(7 + i, 0, n)
        DMA("gpsimd", dst, wall_d[:, o:o + n], [], [key], key)
        return key, dst.rearrange("p (k c) -> p k c", k=8)

    DMA("sync", par[:, :], par_d, [], ["par"], "par")
    DMA("gpsimd", cstb[:, :], cst_d[:, 0:NCB], [], ["cst"], "cst")
    for k in range(8):
        DMA("sync", hT[:, k * T:(k + 1) * T], h0_d[:, k * T:(k + 1) * T], [],
            ["h%d" % i for i in range(5)], "h0_%d" % k)
    V(lambda e: e.memset(zb[:, :], 0.0), [], ["zb"])
    scT3 = scT[:, :].rearrange("p (k c) -> p k c", c=2)
    act(scT3[:, :, 0], P_("c"), AF.Silu, ["par"], ["scT"])
    act(scT3[:, :, 1], P_("cc"), AF.Silu, ["par"], ["scT"])
    ident = C_("ident")
    ones_bf = C_("ones")

    def ada_steps(l):
        mk = "mod%d" % l
        mT = modT[l]
        m3 = mT[:, :].rearrange("p (j c) -> p j c", c=2)
        ab = P_("ab%d" % l)
        w1v = w1vs[l]

        def step(t):
            pk, pb = bank()
            wk, wb = wload("ada%d_%d" % (l, t))
            for jj in range(4):
                for k in range(8):
                    mm(pb[:, 2 * jj:2 * jj + 2], wb[:, k, jj * 128:(jj + 1) * 128], scT3[:, k, :],
                       k == 0, k == 7, [wk, "scT"], [pk])
            tt("vector", m3[:, 4 * t:4 * t + 4, :], pb[:, 0:8].rearrange("p (j c) -> p j c", c=2),
               ab[:, 4 * t:4 * t + 4].unsqueeze(2).to_broadcast([128, 4, 2]), ALU.add, [pk, "par"], [mk])

        def fin():
            for c in range(2):
                ts("vector", w1v[:, c * 8:(c + 1) * 8], m3[:, 8:16, c], 1.0, None, ALU.add, None, [mk], ["w1v%d" % l])
                tt("vector", w1v[:, c * 8:(c + 1) * 8], w1v[:, c * 8:(c + 1) * 8], P_("nw%d" % l), ALU.mult,
                   ["w1v%d" % l, "par"], ["w1v%d" % l])
        return [lambda t=t: step(t) for t in range(6)] + [fin]

    ada_info = {}

    def ada_get(l):
        mT = modT[l]
        return mT[:, :].rearrange("p (j c) -> p j c", c=2), "mod%d" % l

    pending_ada = []

    def ada_pump(n=1):
        for _ in range(n):
            if pending_ada:
                pending_ada.pop(0)()

    def norm(l, m3, mk):
        for ti, (a, b) in enumerate(TT):
            w = b - a
            c = 1 if ti == 0 else 0
            sq = sbf(ti % 2, 0, 8 * 512).rearrange("p (k n) -> p k n", k=8)
            sqk = "s%d" % (ti % 2)
            act(sq[:, :, :w], hT3[:, :, a:b], AF.Square, ["h%d" % ti], [sqk])
            pk, pb = bank()
            for k in range(8):
                mm(pb[:, :w], ones_bf, sq[:, k, :w], k == 0, k == 7, [sqk, "cst"], [pk])
            rs = sf(2, (ti % 2) * 512, (ti % 2) * 512 + w)
            rk = "s2r%d" % (ti % 2)
            ts("vector", rs, pb[:, :w], 1.0 / 1024, 1e-6, ALU.mult, ALU.add, [pk], [rk])
            act(rs, rs, AF.Sqrt, [rk], [rk])
            V(lambda e, rs=rs: e.reciprocal(out=rs, in_=rs), [rk], [rk])
            for k in range(8):
                tmp = sf(3, (k % 2) * 512, (k % 2) * 512 + w)
                tk = "s3t%d" % (k % 2)
                stt(tmp, hT3[:, k, a:b], w1vs[l][:, c * 8 + k:c * 8 + k + 1], rs, ALU.mult, ALU.mult,
                    ["h%d" % ti, "w1v%d" % l, rk], [tk])
                act(uT3[:, k, a:b], tmp, AF.Identity, [tk, mk], ["u%d" % ti], bias=m3[:, k, c:c + 1])

    def proj(wb, wk, col0, M, ti, pb, pk, width=None, a0=None):
        a, b = TT[ti]
        if a0 is not None:
            a, b = a0, a0 + width
        for k in range(8):
            mm(pb[0:M, :b - a], wb[:, k, col0:col0 + M], uT3[:, k, a:b], k == 0, k == 7,
               [wk, "u%d" % ti], [pk])

    def outproj(names, k0, m3, mk):
        wts = [wload(n) for n in names]
        for ti, (a, b) in enumerate(TT):
            w = b - a
            c = 1 if ti == 0 else 0
            mr = sbf(ti % 2, 0, 8 * 512).rearrange("p (k n) -> p k n", k=8)
            mrk = "s%d" % (ti % 2)
            DMA("sync", mr[:, :, :w], mixD[k0:k0 + 8].rearrange("k p n -> p k n")[:, :, a:b],
                ["mixD"], [mrk], mrk)
            for j in range(8):
                wk, wb = wts[j // 4]
                pk, pb = bank()
                for k in range(8):
                    mm(pb[:, :w], wb[:, k, (j % 4) * 128:(j % 4 + 1) * 128], mr[:, k, :w], k == 0, k == 7,
                       [wk, mrk], [pk])
                stt(hT3[:, j, a:b], pb[:, :w], m3[:, 16 + j, c:c + 1], hT3[:, j, a:b], ALU.mult, ALU.add,
                    [pk, mk, "h%d" % ti], ["h%d" % ti])

    SEG = [(1, 257)] + [(260 + 512 * i, 260 + 512 * (i + 1)) for i in range(4)]

    def rg_setup(j):
        for d in range(2):
            t0 = sm[:, 32 + d * 8:40 + d * 8]
            act(t0, P_("lam%d%d" % (j, d)), AF.Exp, ["par"], ["smt"], scale=-1.0)
            ts("vector", t0, t0, 1.0, None, ALU.add, None, ["smt"], ["smt"])
            act(t0, t0, AF.Ln, ["smt"], ["smt"])
            ts("vector", sm[:, d * 16:d * 16 + 8], t0, -8.0, None, ALU.mult, None, ["smt"], ["rgc"])
            ts("vector", sm[:, 640 + d * 8:648 + d * 8], t0, -4.0, None, ALU.mult, None, ["smt"], ["rgc"])
            ts("vector", sm[:, 656 + d * 8:664 + d * 8], P_("bx%d%d" % (j, d)), 0.5, None, ALU.mult, None, ["par"], ["rgc"])
            ts("vector", sm[:, 672 + d * 8:680 + d * 8], P_("ba%d%d" % (j, d)), 0.5, None, ALU.mult, None, ["par"], ["rgc"])
        V(lambda e: e.memset(sm[:, 690:691], 0.25), [], ["rgc"])

    def rglru(j, h, wk, wb, colx, colg):
        xa, uc, hf, hb = sf(0), sf(1), sf(3), sf(4)
        ucb = sbf(2, 0, SLOTW)
        mst = sbf(2, SLOTW, SLOTW + T)
        rk, rw = "rgw%d" % (h % 2), rgw[h % 2]
        o, n = WL.off["rg%d_%d" % (j, h)]
        DMA("gpsimd", rw[:, :], wall_d[:, o:o + n], [], [rk], rk)
        rw3 = rw[:, :].rearrange("p (m c) -> p m c", m=4)
        for (a, b) in ((0, 1), (257, 260), (2308, 2312)):
            V(lambda e, a=a, b=b: e.memset(xa[:, a:b], 0.0), [], ["s0"])
        for ti in range(5):
            pk, pb = bank()
            proj(wb, wk, colx, 128, ti, pb, pk)
            a, b = SEG[ti]
            V(lambda e, a=a, b=b, pb=pb: e.tensor_copy(out=xa[:, a:b], in_=pb[:, :b - a]), [pk], ["s0"])
        cw = P_("cw%d" % j)
        ts("vector", uc[:, 1:2308], xa[:, 0:2307], cw[:, h:h + 1], P_("cb%d" % j, h, h + 1), ALU.mult, ALU.add,
           ["s0", "par"], ["s1"])
        for tap in range(1, 4):
            stt(uc[:, 1:2308], xa[:, tap:tap + 2307], cw[:, tap * 8 + h:tap * 8 + h + 1], uc[:, 1:2308],
                ALU.mult, ALU.add, ["s0", "par", "s1"], ["s1"])
        act(ucb[:, 1:2308], uc[:, 1:2308], AF.Copy, ["s1"], ["s2a"])
        RNG = ((1, 257), (260, 2308))
        thx, aa, mt = sf(0), sf(5), sf(6)
        for d in range(2):
            hd, hk = (hf, "s3") if d == 0 else (hb, "s4")
            c1 = sm[:, d * 16 + h:d * 16 + h + 1]
            hc1 = sm[:, 640 + d * 8 + h:641 + d * 8 + h]
            hbx = sm[:, 656 + d * 8 + h:657 + d * 8 + h]
            hba = sm[:, 672 + d * 8 + h:673 + d * 8 + h]
            for (a, b) in SEG:
                w = b - a
                pkx, pbx = bank()
                mm(pbx[:, :w], rw3[:, 2 * d, :], ucb[:, a:b], True, True, [rk, "s2a"], [pkx])
                pka, pba = bank()
                mm(pba[:, :w], rw3[:, 2 * d + 1, :], ucb[:, a:b], True, True, [rk, "s2a"], [pka])
                act(thx[:, a:b], pbx[:, :w], AF.Tanh, [pkx, "rgc"], ["s0"], scale=0.5, bias=hbx)
                act(aa[:, a:b], pba[:, :w], AF.Tanh, [pka, "rgc"], ["s5"], scale=0.5, bias=hba)
                act(mt[:, a:b], aa[:, a:b], AF.Exp, ["s5", "rgc"], ["s6"], scale=c1, bias=c1)
                act(aa[:, a:b], aa[:, a:b], AF.Exp, ["s5", "rgc"], ["s5"], scale=hc1, bias=hc1)
            for (a, b) in RNG:
                act(mt[:, a:b], mt[:, a:b], AF.Sqrt, ["s6", "rgc"], ["s6"], scale=-0.25, bias=sm[:, 690:691])
                stt(thx[:, a:b], thx[:, a:b], 1.0, mt[:, a:b], ALU.add, ALU.mult, ["s0", "s6"], ["s0"])
                tt("vector", thx[:, a:b], thx[:, a:b], uc[:, a:b], ALU.mult, ["s0", "s1"], ["s0"])
            if d == 0:
                V(lambda e: e.tensor_tensor_scan(out=hf[:, 1:257], data0=aa[:, 1:257], data1=thx[:, 1:257],
                                                 initial=0.0, op0=ALU.mult, op1=ALU.add), ["s5", "s0"], ["s3"])
                V(lambda e: e.tensor_tensor_scan(out=hf[:, 260:2308], data0=aa[:, 260:2308], data1=thx[:, 260:2308],
                                                 initial=hf[:, 256:257], op0=ALU.mult, op1=ALU.add),
                  ["s5", "s0", "s3"], ["s3"])
            else:
                V(lambda e: e.tensor_tensor_scan(out=hb[:, 256:0:-1], data0=aa[:, 256:0:-1], data1=thx[:, 256:0:-1],
                                                 initial=0.0, op0=ALU.mult, op1=ALU.add), ["s5", "s0"], ["s4"])
                V(lambda e: e.tensor_tensor_scan(out=hb[:, 2307:259:-1], data0=aa[:, 2307:259:-1],
                                                 data1=thx[:, 2307:259:-1], initial=hb[:, 1:2],
                                                 op0=ALU.mult, op1=ALU.add), ["s5", "s0", "s4"], ["s4"])
        tt("vector", hb[:, 1:257], hb[:, 1:257], hf[:, 1:257], ALU.add, ["s3", "s4"], ["s4"])
        tt("vector", hb[:, 260:2308], hb[:, 260:2308], hf[:, 260:2308], ALU.add, ["s3", "s4"], ["s4"])
        for ti in range(5):
            pk, pb = bank()
            proj(wb, wk, colg, 128, ti, pb, pk)
            a, b = SEG[ti]
            ta, tb = TT[ti]
            w = b - a
            th = sf(5, (ti % 2) * 512, (ti % 2) * 512 + w)
            thk = "s5"
            act(th, pb[:, :w], AF.Tanh, [pk], [thk], scale=0.5)
            stt(th, th, 1.0, pb[:, :w], ALU.add, ALU.mult, [pk, thk], [thk])
            stt(mst[:, ta:tb], th, 0.5, hb[:, a:b], ALU.mult, ALU.mult, [thk, "s4"], ["s2b"])
        DMA("sync", mixD[h], mst, ["s2b"], ["mixD"], "mixst")

    def attn_setup(j):
        oC, nC = CL.off["ropeC"]
        oS, nS = CL.off["ropeS"]
        DMA("gpsimd", sbf(0, 0, 2048), cst_d[:, oC:oC + 2048], [], ["s0"], "g_rope")
        DMA("gpsimd", sbf(0, 2048, 4096), cst_d[:, oS:oS + 2048], [], ["s0"], "g_rope")
        act(sm[:, 64:72], P_("snk%d" % j), AF.Exp, ["par"], ["snk"])

    def attn(j, g):
        w1k, w1b = wload("evB1_%d_%d" % (j, g))
        w2k, w2b = wload("evB2_%d_%d" % (j, g))
        RC = sbf(0, 0, 2048)
        RS = sbf(0, 2048, 4096)
        QT = sbf(1, 0, 2 * T).rearrange("p (c n) -> p c n", c=2)
        KT = sbf(2, 0, T)
        VX = sbf(2, T, 2 * T).rearrange("p (t c) -> p t c", c=128)
        SG = sbf(3, 0, 2 * T).rearrange("p (c n) -> p c n", c=2)
        VX2 = sbf(4, 4 * 512, 4 * 512 + T).rearrange("p (t c) -> p t c", c=128)
        MST = sbf(6, 0, 2 * T).rearrange("p (c n) -> p c n", c=2)
        for ti, (a, b) in enumerate(TT):
            w = b - a
            for which in range(3):
                col0 = 128 + which * 128 if which < 2 else 0
                dst = QT[:, which, a:b] if which < 2 else KT[:, a:b]
                dk = "s1" if which < 2 else "s2k"
                pk, pb = bank()
                proj(w1b, w1k, col0, 128, ti, pb, pk)
                if ti == 0:
                    act(dst, pb[:, :w], AF.Copy, [pk], [dk])
                else:
                    la = a - 256
                    qb_ = sbf(5, 4096, 4096 + w)
                    act(qb_, pb[:, :w], AF.Copy, [pk], ["s5q"])
                    pk2, pb2 = bank()
                    mm(pb2[:, :w], C_("pswap"), qb_, True, True, ["s5q", "cst"], [pk2])
                    t1 = sf(5, 0, w)
                    t2 = sf(5, 512, 512 + w)
                    tt("vector", t1, pb[:, :w], RC[:, la:la + w], ALU.mult, [pk, "s0"], ["s5a"])
                    tt("vector", t2, pb2[:, :w], RS[:, la:la + w], ALU.mult, [pk2, "s0"], ["s5b"])
                    tt("vector", dst, t1, t2, ALU.add, ["s5a", "s5b"], [dk])
            for c2 in range(2):
                pk, pb = bank()
                proj(w2b, w2k, c2 * 128, 128, ti, pb, pk)
                act(SG[:, c2, a:b], pb[:, :w], AF.Tanh, [pk], ["s3"], scale=0.5)
                stt(SG[:, c2, a:b], SG[:, c2, a:b], 1.0, pb[:, :w], ALU.add, ALU.mult, [pk, "s3"], ["s3"])
        if KSTOP == "proj":
            return
        V(lambda e: e.memset(VX[:, :, 64:128], 1.0), [], ["s2v"])
        V(lambda e: e.memset(VX2[:, :, 0:64], 1.0), [], ["s4v"])
        for t0 in range(0, 18, 8):
            nt = min(8, 18 - t0)
            pk, pb = bank()
            for t_ in range(nt):
                tk = t0 + t_
                ti = 0 if tk < 2 else 1 + (tk - 2) // 4
                for k in range(8):
                    mm(pb[:, t_ * 64:(t_ + 1) * 64], uT3[:, k, tk * 128:(tk + 1) * 128], w1b[:, k, 384:448],
                       k == 0, k == 7, [w1k, "u%d" % ti], [pk])
            src = pb[:, 0:nt * 64].rearrange("p (t c) -> p t c", c=64)
            V(lambda e, src=src, t0=t0, nt=nt: e.tensor_copy(out=VX[:, t0:t0 + nt, 0:64], in_=src), [pk], ["s2v"])
            act(VX2[:, t0:t0 + nt, 64:128], src, AF.Copy, [pk], ["s4v"])
        if KSTOP == "v":
            return
        blocks = [(0, [(0, None), (1, None)]), (128, [(0, None), (1, None)])]
        for qb in range(16):
            keys = [(0, None), (1, None)]
            if qb > 0:
                keys.append((1 + qb, None if KSTOP == "nomask" else "mprev"))
            keys.append((2 + qb, None))
            if qb < 15:
                keys.append((3 + qb, None if KSTOP == "nomask" else "mnext"))
            blocks.append((256 + qb * 128, keys))
        snk = sm[:, 64 + 2 * g:64 + 2 * g + 2]
        sk3 = snk.unsqueeze(2).to_broadcast([128, 2, 128])
        items = []
        for bi, (q0, keys) in enumerate(blocks):
            for idx, (kt, mt) in enumerate(keys):
                items.append((bi, q0, idx, len(keys), kt, mt))
        sc_ctr = [0]

        def score_bank():
            i = sc_ctr[0] % 4
            sc_ctr[0] += 1
            return "ps%d" % i, ps[i]

        def emit_scores(n):
            bi, q0, idx, nk, kt, mt = items[n]
            pt = sbf(4, (n % 4) * 512, (n % 4 + 1) * 512)
            ptk = "s4p%d" % (n % 4)
            for half in range(2):
                psk, ps_ = score_bank()
                if mt is not None:
                    mm(ps_[:, 0:256], ident, C_(mt, 0, 256), True, False, ["cst"], [psk])
                mm(ps_[:, 0:256].rearrange("p (c n) -> p c n", c=2),
                   KT[half * 64:(half + 1) * 64, kt * 128:(kt + 1) * 128],
                   QT[half * 64:(half + 1) * 64, :, q0:q0 + 128], mt is None, True, ["s2k", "s1"], [psk])
                act(pt[:, half * 256:(half + 1) * 256], ps_[:, 0:256], AF.Exp, [psk], [ptk], scale=0.125)

        def emit_pv(n):
            bi, q0, idx, nk, kt, mt = items[n]
            pt = sbf(4, (n % 4) * 512, (n % 4 + 1) * 512)
            ptk = "s4p%d" % (n % 4)
            ia, ib = (4, 5) if bi % 2 == 0 else (6, 7)
            pek, pe_ = "ps%d" % ia, ps[ia]
            pok, po_ = "ps%d" % ib, ps[ib]
            mm(pe_[:, 0:256], VX[:, kt, :], pt[:, 0:256], idx == 0, idx == nk - 1, ["s2v", ptk], [pek])
            mm(po_[:, 0:256], VX2[:, kt, :], pt[:, 256:512], idx == 0, idx == nk - 1, ["s4v", ptk], [pok])
            if idx != nk - 1:
                return
            q = bi % 2
            rd = sf(5, 1024 + q * 512, 1280 + q * 512)
            t1 = sf(5, 1280 + q * 512, 1536 + q * 512)
            rk, tk = "s5r%d" % q, "s5t%d" % q
            tt("vector", rd[64:128, :].rearrange("p (c n) -> p c n", c=2),
               pe_[64:128, 0:256].rearrange("p (c n) -> p c n", c=2), sk3[64:128], ALU.add, [pek, "snk"], [rk])
            tt("vector", rd[0:64, :].rearrange("p (c n) -> p c n", c=2),
               po_[0:64, 0:256].rearrange("p (c n) -> p c n", c=2), sk3[0:64], ALU.add, [pok, "snk"], [rk])
            V(lambda e, rd=rd: e.reciprocal(out=rd, in_=rd), [rk], [rk])
            tt("vector", t1[0:64, :], pe_[0:64, 0:256], rd[64:128, :], ALU.mult, [pek, rk], [tk])
            tt("vector", t1[64:128, :], po_[64:128, 0:256], rd[0:64, :], ALU.mult, [pok, rk], [tk])
            stt(MST[:, :, q0:q0 + 128], t1.rearrange("p (c n) -> p c n", c=2), 0.5, SG[:, :, q0:q0 + 128],
                ALU.mult, ALU.mult, [tk, "s3"], ["s6"])

        for n in range(len(items) + 1):
            if n < len(items):
                emit_scores(n)
            if n >= 1:
                emit_pv(n - 1)
        for c2 in range(2):
            DMA("sync", mixD[8 + 2 * g + c2], MST[:, c2, :], ["s6"], ["mixD"], "mixst")

    def layer_begin(l):
        ada_pump(100)
        if l + 1 < nlayers:
            pending_ada.extend(ada_steps(l + 1))
        return ada_get(l)

    def even_layer(l):
        j = l // 2
        m3, mk = layer_begin(l)
        S.fence()
        norm(l, m3, mk)
        S.fence()
        rg_setup(j)
        for i in range(4):
            wk, wb = wload("evA%d_%d" % (j, i))
            for hh in range(2):
                rglru(j, 2 * i + hh, wk, wb, hh * 256, hh * 256 + 128)
            ada_pump()
        S.fence()
        outproj(["evO%d_0_0" % j, "evO%d_0_1" % j], 0, m3, mk)
        S.fence()
        if os.environ.get("KSKIP") == "attn":
            return
        attn_setup(j)
        for g in range(1 if KSTOP else 4):
            attn(j, g)
            ada_pump()
        S.fence()
        if KSTOP:
            return
        outproj(["evO%d_1_0" % j, "evO%d_1_1" % j], 8, m3, mk)
        S.fence()

    CH = 64
    NCH = T // CH

    def odd_head(j, hd, m3, mk):
        w1k, w1b = wload("od1_%d_%d" % (j, hd))
        w2k, w2b = wload("od2_%d_%d" % (j, hd))
        qs = sf(0)
        vtok = sbf(1, 0, NCH * 128).rearrange("p (t c) -> p t c", c=128)
        vtile = vtile_t[:, :]
        fbuf = sf(2)
        one_b = P_("one").to_broadcast([128, T])
        lb = sm[:, 80 + hd:81 + hd]
        oml = sm[:, 88 + hd:89 + hd]
        for ti, (a, b) in enumerate(TT):
            w = b - a
            pk, pb = bank()
            proj(w1b, w1k, 384, 128, ti, pb, pk)
            act(qs[:, a:b], pb[:, :w], AF.Tanh, [pk], ["s0"], scale=0.5)
            stt(qs[:, a:b], qs[:, a:b], 1.0, pb[:, :w], ALU.add, ALU.mult, [pk, "s0"], ["s0"])
            pk, pb = bank()
            proj(w1b, w1k, 256, 128, ti, pb, pk)
            act(vtile[:, :w], pb[:, :w], AF.Copy, [pk], ["vtile"])
            pk2, pb2 = bank()
            pbT = pb2[:, :].bitcast(BF16)
            nb = w // CH
            for i in range(nb):
                PE(lambda e, i=i, pbT=pbT: e.transpose(pbT[0:CH, i * 128:(i + 1) * 128], vtile[:, i * CH:(i + 1) * CH], ident),
                   ["vtile", "cst"], [pk2])
            c0 = a // CH
            V(lambda e, pbT=pbT, c0=c0, nb=nb: e.tensor_copy(
                out=vtok[0:CH, c0:c0 + nb, :], in_=pbT[0:CH, 0:nb * 128].rearrange("p (t c) -> p t c", c=128)),
              [pk2], ["s1v"])
        qk = {}
        for d in range(2):
            gbuf, Bg, Em = sf(3), sf(4), sf(5)
            sv = 128 + d * 256
            Bst, Bmid, tA = sm[:, sv:sv + NCH], sm[:, sv + NCH:sv + 2 * NCH], sm[:, sv + 2 * NCH:sv + 3 * NCH]
            e1, esn, e2n = (sm[:, sv + 3 * NCH:sv + 4 * NCH], sm[:, sv + 4 * NCH:sv + 5 * NCH],
                            sm[:, sv + 5 * NCH:sv + 6 * NCH])
            svk = "smo%d" % d
            for ti, (a, b) in enumerate(TT):
                pk, pb = bank()
                proj(w1b, w1k, d * 128, 128, ti, pb, pk)
                act(fbuf[:, a:b], pb[:, :b - a], AF.Tanh, [pk], ["s2"], scale=0.5)
            ts("vector", fbuf[:, 0:T], fbuf[:, 0:T], oml, lb, ALU.mult, ALU.add, ["s2", "lbv"], ["s2"])
            act(gbuf[:, 0:T], fbuf[:, 0:T], AF.Ln, ["s2"], ["s3"])
            if d == 0:
                V(lambda e, Bg=Bg, gbuf=gbuf: e.tensor_tensor_scan(
                    out=Bg[:, 0:T], data0=one_b, data1=gbuf[:, 0:T], initial=0.0, op0=ALU.mult, op1=ALU.add),
                  ["s3", "par"], ["s4"])
                Bend = Bg[:, CH - 1:T:CH]
            else:
                V(lambda e, Bg=Bg, gbuf=gbuf: e.tensor_tensor_scan(
                    out=Bg[:, T - 1::-1], data0=one_b, data1=gbuf[:, T - 1::-1], initial=0.0,
                    op0=ALU.mult, op1=ALU.add), ["s3", "par"], ["s4"])
                Bend = Bg[:, 0:T:CH]
            V(lambda e, Bmid=Bmid, Bg=Bg: e.tensor_copy(out=Bmid, in_=Bg[:, CH // 2:T:CH]), ["s4"], [svk])
            if d == 0:
                V(lambda e, Bst=Bst: e.memset(Bst[:, 0:1], 0.0), [], [svk])
                V(lambda e, Bst=Bst, Bend=Bend: e.tensor_copy(out=Bst[:, 1:NCH], in_=Bend[:, 0:NCH - 1]), ["s4"], [svk])
            else:
                V(lambda e, Bst=Bst: e.memset(Bst[:, NCH - 1:NCH], 0.0), [], [svk])
                V(lambda e, Bst=Bst, Bend=Bend: e.tensor_copy(out=Bst[:, 0:NCH - 1], in_=Bend[:, 1:NCH]), ["s4"], [svk])
            tt("vector", tA, Bend, Bst, ALU.subtract, ["s4", svk], [svk + "t"])
            act(e1, tA, AF.Exp, [svk + "t"], [svk + "e"])
            tt("vector", tA, Bmid, Bst, ALU.subtract, [svk, svk + "e"], [svk + "t"])
            act(esn, tA, AF.Exp, [svk + "t"], [svk + "e"])
            tt("vector", tA, Bend, Bmid, ALU.subtract, ["s4", svk, svk + "e"], [svk + "t"])
            act(e2n, tA, AF.Exp, [svk + "t"], [svk + "e"])
            ts("vector", e2n, e2n, -1.0, None, ALU.mult, None, [svk + "e"], [svk + "e"])
            ts("vector", esn, esn, -1.0, None, ALU.mult, None, [svk + "e"], [svk + "e"])
            d3 = gbuf[:, 0:T].rearrange("p (c n) -> p c n", n=CH)
            tt("vector", d3, Bg[:, 0:T].rearrange("p (c n) -> p c n", n=CH),
               Bmid.unsqueeze(2).to_broadcast([128, NCH, CH]), ALU.subtract, ["s4", svk], ["s3"])
            act(Bg[:, 0:T], gbuf[:, 0:T], AF.Exp, ["s3"], ["s4"])
            act(Em[:, 0:T], gbuf[:, 0:T], AF.Exp, ["s3"], ["s5"], scale=-1.0)
            ds = 6 if d == 0 else 3
            qt = sbf(ds, 0, T)
            kt = sbf(ds, T, 2 * T)
            dk = "s%d" % ds
            stt(qt, qs[:, 0:T], -0.5, Bg[:, 0:T], ALU.mult, ALU.mult, ["s0", "s4"], [dk])
            stt(kt, fbuf[:, 0:T], 1.0, Em[:, 0:T], ALU.subtract, ALU.mult, ["s2", "s5"], [dk])
            qk[d] = (qt, kt, dk, e1, esn, e2n, svk + "e")
        osum = sf(0)
        okall = ["s0o%d" % c for c in range(NCH)]
        S.add("vector", lambda e: e.memset(dummy[:, :], 0.0), [], ["s0"] + okall)
        for x_ in range(6):
            V(lambda e, x_=x_: e.memset(atm_t[x_][:, :], 0.0), [], ["atm%d" % x_])
        nctx = 256 // CH
        order = {0: list(range(NCH)), 1: list(range(nctx - 1, -1, -1)) + list(range(NCH - 1, nctx - 1, -1))}
        step_of = {d: {c: i for i, c in enumerate(order[d])} for d in range(2)}
        ob_ctr = [0]

        def obank():
            i = ob_ctr[0] % 8
            ob_ctr[0] += 1
            return "ps%d" % i, ps[i]

        Hh = CH // 2

        def part1(i, d):
            qt, kt, dk, e1, esn, e2n, sek = qk[d]
            c = order[d][i]
            c0_, cm_, c1_ = c * CH, c * CH + Hh, (c + 1) * CH
            x_ = d * 3 + i % 3
            pka, pa = obank()
            if d == 0:
                mm(pa[0:CH, Hh:CH], kt[:, c0_:c1_], qt[:, cm_:c1_], True, True, [dk], [pka])
                mm(pa[0:Hh, 0:Hh], kt[:, c0_:cm_], qt[:, c0_:cm_], True, True, [dk], [pka])
                mm(pa[Hh:CH, 0:Hh], kt[:, cm_:c1_], zb[:, 0:Hh], True, True, [dk, "zb"], [pka])
            else:
                mm(pa[0:CH, 0:Hh], kt[:, c0_:c1_], qt[:, c0_:cm_], True, True, [dk], [pka])
                mm(pa[Hh:CH, Hh:CH], kt[:, cm_:c1_], qt[:, cm_:c1_], True, True, [dk], [pka])
                mm(pa[0:Hh, Hh:CH], kt[:, c0_:cm_], zb[:, 0:Hh], True, True, [dk, "zb"], [pka])
            am = atm_t[x_][:, :]
            mo, mw = CL.off["maskF" if d == 0 else "maskB"]
            msk = cstu[0:CH, mo:mo + CH]
            V(lambda e, am=am, msk=msk, pa=pa: e.copy_predicated(out=am, mask=msk, data=pa[0:CH, 0:CH]),
              [pka, "cst"], ["atm%d" % x_])
            pkt, pt_ = obank()
            ptT = pt_[:, :].bitcast(BF16)
            PE(lambda e, ptT=ptT, kt=kt, c0_=c0_, c1_=c1_: e.transpose(ptT[0:CH, 0:128], kt[:, c0_:c1_], ident),
               [dk, "cst"], [pkt])
            act(ktok_t[x_][:, :], ptT[0:CH, 0:128], AF.Copy, [pkt], ["ktok%d" % x_])

        def part2(i, d):
            qt, kt, dk, e1, esn, e2n, sek = qk[d]
            c = order[d][i]
            cs = slice(c * CH, (c + 1) * CH)
            first, last = i == 0, i == NCH - 1
            x_ = d * 3 + i % 3
            am, amk = atm_t[x_][:, :], "atm%d" % x_
            kk, kkk = ktok_t[x_][:, :], "ktok%d" % x_
            pko, po = obank()
            if not first:
                mm(po[:, 0:CH], Sbf[d][:, :], qt[:, cs], True, False, ["Sbf%d" % d, dk], [pko])
            mm(po[:, 0:CH], vtok[0:CH, c, :], am, first, True, ["s1v", amk], [pko])
            ok = "s0o%d" % c
            if step_of[1 - d][c] > i or (step_of[1 - d][c] == i and d == 0):
                act(osum[:, cs], po[:, 0:CH], AF.Copy, [pko], [ok])
            else:
                tt("vector", osum[:, cs], po[:, 0:CH], osum[:, cs], ALU.add, [pko, ok], [ok])
            pku, pu = obank()
            mm(pu[:, 0:128], kk, vtok[0:CH, c, :], True, True, [kkk, "s1v"], [pku])
            sk = "Sst%d" % d
            if first:
                ts("vector", Sst[d][:, :], pu[:, 0:128], e2n[:, c:c + 1], None, ALU.mult, None, [pku, sek], [sk])
            else:
                ts("vector", Stm[d][:, :], Sst[d][:, :], e1[:, c:c + 1], None, ALU.mult, None, [sk, sek],
                   ["Stm%d" % d])
                stt(Sst[d][:, :], pu[:, 0:128], e2n[:, c:c + 1], Stm[d][:, :], ALU.mult, ALU.add,
                    [pku, sek, "Stm%d" % d], [sk])
            if not last:
                cn = order[d][i + 1]
                act(Sbf[d][:, :], Sst[d][:, :], AF.Copy, [sk, sek], ["Sbf%d" % d], scale=esn[:, cn:cn + 1])

        flat = [(i, d) for i in range(NCH) for d in range(2)]
        LOOK = 2
        for n in range(len(flat) + LOOK):
            if n < len(flat):
                part1(*flat[n])
            if n >= LOOK:
                part2(*flat[n - LOOK])
        okeys = okall
        S.fence()
        mst = sbf(2, SLOTW, SLOTW + T)
        gn = P_("gn%d" % j)
        for ti, (a, b) in enumerate(TT):
            w = b - a
            q = ti % 2
            sq = sbf(2, q * 512, q * 512 + w)
            sqk = "s2q%d" % q
            act(sq, osum[:, a:b], AF.Square, okeys, [sqk])
            pk, pb = bank()
            mm(pb[:, :w], ones_bf, sq, True, True, [sqk, "cst"], [pk])
            rs = sf(6, q * 512, q * 512 + w)
            rk = "s6r%d" % q
            ts("vector", rs, pb[:, :w], 1.0 / 128, 1e-6, ALU.mult, ALU.add, [pk], [rk])
            act(rs, rs, AF.Sqrt, [rk], [rk])
            V(lambda e, rs=rs: e.reciprocal(out=rs, in_=rs), [rk], [rk])
            stt(osum[:, a:b], osum[:, a:b], gn[:, 0:1], rs, ALU.mult, ALU.mult, okeys + ["par", rk], ["s0y"])
        for ti, (a, b) in enumerate(TT):
            w = b - a
            q = ti % 2
            pk, pb = bank()
            proj(w2b, w2k, 0, 128, ti, pb, pk)
            sg = sf(6, 1024 + q * 512, 1024 + q * 512 + w)
            sgk = "s6g%d" % q
            act(sg, pb[:, :w], AF.Tanh, [pk], [sgk], scale=0.5)
            stt(sg, sg, 1.0, pb[:, :w], ALU.add, ALU.mult, [pk, sgk], [sgk])
            stt(mst[:, a:b], sg, 0.5, osum[:, a:b], ALU.mult, ALU.mult, [sgk, "s0y"], ["s2m"])
        DMA("sync", mixD[hd], mst, ["s2m"], ["mixD"], "mixst")

    def odd_layer(l):
        j = l // 2
        m3, mk = layer_begin(l)
        S.fence()
        norm(l, m3, mk)
        S.fence()
        if j == 0:
            V(lambda e: e.memset(sm[:, 80:88], 0.5), [], ["lbv"])
            V(lambda e: e.memset(sm[:, 88:96], 0.5), [], ["lbv"])
        else:
            tt("vector", sm[:, 96:104], P_("lbr1"), P_("lbr0"), ALU.subtract, ["par"], ["lbt"])
            act(sm[:, 96:104], sm[:, 96:104], AF.Sigmoid, ["lbt"], ["lbt"])
            ts("vector", sm[:, 88:96], sm[:, 96:104], -0.5, 0.5, ALU.mult, ALU.add, ["lbt"], ["lbv"])
            tt("vector", sm[:, 80:88], sm[:, 96:104], sm[:, 88:96], ALU.add, ["lbt", "lbv"], ["lbv"])
        for hd in range(8):
            odd_head(j, hd, m3, mk)
            S.fence()
            ada_pump()
        outproj(["odO%d_0" % j, "odO%d_1" % j], 0, m3, mk)
        S.fence()

    pending_ada.extend(ada_steps(0))
    for l in range(nlayers):
        if l % 2 == 0:
            even_layer(l)
        else:
            odd_layer(l)

    S.fence()
    for ti in range(1, 5):
        a, b = TT[ti]
        w = b - a
        sq = sbf(ti % 2, 0, 8 * 512).rearrange("p (k n) -> p k n", k=8)
        sqk = "s%d" % (ti % 2)
        act(sq[:, :, :w], hT3[:, :, a:b], AF.Square, ["h%d" % ti], [sqk])
        pk, pb = bank()
        for k in range(8):
            mm(pb[:, :w], ones_bf, sq[:, k, :w], k == 0, k == 7, [sqk, "cst"], [pk])
        rs = sf(2, (ti % 2) * 512, (ti % 2) * 512 + w)
        rk = "s2r%d" % (ti % 2)
        ts("vector", rs, pb[:, :w], 1.0 / 1024, 1e-6, ALU.mult, ALU.add, [pk], [rk])
        act(rs, rs, AF.Sqrt, [rk], [rk])
        V(lambda e, rs=rs: e.reciprocal(out=rs, in_=rs), [rk], [rk])
        fn = P_("fnw")
        for k in range(8):
            stt(hT3[:, k, a:b], hT3[:, k, a:b], fn[:, k:k + 1], rs, ALU.mult, ALU.mult,
                ["h%d" % ti, "par", rk], ["h%d" % ti])
        for k in range(8):
            DMA("sync", out_d[:, k * 2048 + a - 256:k * 2048 + b - 256], hT3[:, k, a:b], ["h%d" % ti], [],
                "out%d" % k)
    S.emit()
    return nc


_CACHE = {}


def kernel(**inp):
    inp = {k: np.asarray(v) for k, v in inp.items()}
    WL = build_weights(inp)
    CL = build_consts()
    wall = WL.build()
    cst = CL.build()
    in_maps = []
    PL = None
    for b in range(8):
        PL = build_params(inp, b)
        h0 = np.concatenate([inp["ctx"][b], inp["x"][b]], axis=0).astype(np.float32)
        h0 = np.ascontiguousarray(h0.T.reshape(8, 128, T).transpose(1, 0, 2).reshape(128, 8 * T))
        in_maps.append({"h0": h0, "par": PL.build(), "wall": wall, "cst": cst})
    nc = build_program(WL, PL, CL, NLAYERS)
    res = run_bass_kernel_spmd(nc, in_maps, core_ids=list(range(8)))
    out = np.empty((8, 2048, 1024), np.float32)
    for b in range(8):
        o = np.asarray(res.results[b]["outT"]).reshape(128, 8, 2048)
        out[b] = o.transpose(2, 1, 0).reshape(2048, 1024)
    return out
```
